# Optimizing a Trainium2 kernel written in Bass

```python
import jax, jax.numpy as jnp
from jax import lax
import numpy as np

D_MODEL = 1024
BATCH = 8
SEQ = 2048
DEPTH = 1
DEC_BATCH = 128
DEC_SEQ = 8
PAST_LEN = 16384
PAGE_SIZE = 128

CHUNK = 128
SGU_GROUPS = 4
SGU_WIDTH = D_MODEL // 2
SGU_GROUP_DIM = SGU_WIDTH // SGU_GROUPS
POOL_WINDOWS = (2, 4, 8, 16)
POOL_GROUPS = len(POOL_WINDOWS)
POOL_WIDTH = D_MODEL // 4
POOL_GROUP_DIM = POOL_WIDTH // POOL_GROUPS
POOL_STATE = max(POOL_WINDOWS) - 1
N_MEM = 256
X_HEADS = 4
X_WIDTH = D_MODEL // 4
X_HEAD_DIM = X_WIDTH // X_HEADS
N_BRANCH = 3
D_FF = 4 * D_MODEL
D_IN = 2 * SGU_WIDTH + POOL_WIDTH + X_WIDTH + N_BRANCH * D_MODEL
EPS = 1e-6

kernel_name = "hybrid_gmlp_pool_memxattn_decode_step"


def rmsnorm(x, g):
    xf = x.astype(jnp.float32)
    y = xf * lax.rsqrt(jnp.mean(xf * xf, axis=-1, keepdims=True) + EPS)
    return (y * g.astype(jnp.float32)).astype(x.dtype)


def layernorm(x, g, b):
    xf = x.astype(jnp.float32)
    mu = jnp.mean(xf, axis=-1, keepdims=True)
    xc = xf - mu
    y = xc * lax.rsqrt(jnp.mean(xc * xc, axis=-1, keepdims=True) + EPS)
    return (y * g.astype(jnp.float32) + b.astype(jnp.float32)).astype(x.dtype)


def chunk_spatial_gate(u, vhat, w_s, b_s):
    bsz, length, _ = u.shape
    n_chunks = -(-length // CHUNK)
    pad = n_chunks * CHUNK - length
    vp = jnp.pad(vhat, ((0, 0), (0, pad), (0, 0)))
    vp = vp.reshape(bsz, n_chunks, CHUNK, SGU_GROUPS, SGU_GROUP_DIM)
    causal = jnp.tril(jnp.ones((CHUNK, CHUNK), dtype=bool))
    w = jnp.where(causal, w_s, 0).astype(vp.dtype)
    mixed = jnp.einsum('gij,bnjgd->bnigd', w, vp) + b_s.T[None, None, :, :, None]
    mixed = mixed.reshape(bsz, n_chunks * CHUNK, SGU_WIDTH)[:, :length]
    return u * mixed


def multiscale_pool(p_prev, p_new, start_pos, w_pool, pool_scale):
    bsz, length, _ = p_new.shape
    ext = jnp.concatenate([p_prev, p_new], axis=1)
    extf = ext.astype(jnp.float32)
    csum = jnp.pad(jnp.cumsum(extf, axis=1), ((0, 0), (1, 0), (0, 0)))
    pos = start_pos + jnp.arange(length)
    means = []
    for gi, win in enumerate(POOL_WINDOWS):
        sl = slice(gi * POOL_GROUP_DIM, (gi + 1) * POOL_GROUP_DIM)
        hi = csum[:, POOL_STATE + 1:POOL_STATE + 1 + length, sl]
        lo = csum[:, POOL_STATE + 1 - win:POOL_STATE + 1 - win + length, sl]
        cnt = jnp.minimum(pos + 1, win).astype(jnp.float32)[None, :, None]
        means.append((hi - lo) / cnt)
    pooled = jnp.concatenate(means, axis=-1) - extf[:, POOL_STATE:]
    pooled = pooled.astype(p_new.dtype).reshape(bsz, length, POOL_GROUPS, POOL_GROUP_DIM)
    mixed = jnp.einsum('blgd,gde->blge', pooled, w_pool).reshape(bsz, length, POOL_WIDTH)
    return mixed * pool_scale, ext[:, -POOL_STATE:]


def memory_kv(mem, g_mem, w_kv):
    bsz = mem.shape[0]
    kv = rmsnorm(mem, g_mem) @ w_kv
    k, v = jnp.split(kv, 2, axis=-1)
    return (k.reshape(bsz, N_MEM, X_HEADS, X_HEAD_DIM),
            v.reshape(bsz, N_MEM, X_HEADS, X_HEAD_DIM))


def memory_attend(q, mem_k, mem_v):
    bsz, length, _ = q.shape
    qh = q.reshape(bsz, length, X_HEADS, X_HEAD_DIM)
    s = jnp.einsum('blhd,bmhd->bhlm', qh, mem_k,
                   preferred_element_type=jnp.float32) * (X_HEAD_DIM ** -0.5)
    p = jax.nn.softmax(s, axis=-1).astype(mem_v.dtype)
    o = jnp.einsum('bhlm,bmhd->blhd', p, mem_v)
    return o.reshape(bsz, length, X_WIDTH)


def decoder_layer(x, pool_prev, start_pos, mem_k, mem_v,
                  g_mix, w_in, g_v, b_v, w_s, b_s, w_pool, pool_scale,
                  w_out_a, w_out_b, w_out_c, w_o, g_ffn, w_up, w_down):
    length = x.shape[1]
    h = rmsnorm(x, g_mix)
    z = h @ w_in
    cuts = [SGU_WIDTH, 2 * SGU_WIDTH, 2 * SGU_WIDTH + POOL_WIDTH,
            2 * SGU_WIDTH + POOL_WIDTH + X_WIDTH]
    u, v, p, q, gate_logits = jnp.split(z, cuts, axis=-1)
    u = jax.nn.gelu(u)
    vhat = layernorm(jax.nn.gelu(v), g_v, b_v)
    a = chunk_spatial_gate(u, vhat, w_s, b_s) @ w_out_a
    pooled, pool_tail = multiscale_pool(pool_prev, p, start_pos, w_pool, pool_scale)
    b = pooled @ w_out_b
    c = memory_attend(q, mem_k, mem_v) @ w_out_c
    gates = jax.nn.sigmoid(gate_logits.astype(jnp.float32)).astype(x.dtype)
    g_a, g_b, g_c = jnp.split(gates, N_BRANCH, axis=-1)
    x = x + (g_a * a + g_b * b + g_c * c) @ w_o
    h2 = rmsnorm(x, g_ffn)
    x = x + jnp.square(jax.nn.relu(h2 @ w_up)) @ w_down
    open_start = ((length - 1) // CHUNK) * CHUNK
    return x, vhat[:, open_start:], pool_tail


def setup_inputs(seed: int = 0) -> dict:
    key = jax.random.key(seed)
    ks = iter(jax.random.split(key, 32))
    nrm = lambda shape, scale=1.0: jax.random.normal(next(ks), shape, jnp.float32) * scale
    gain = lambda shape: 1.0 + 0.05 * jax.random.normal(next(ks), shape, jnp.float32)
    L = DEPTH
    return {
        "x_prompt": nrm((BATCH, SEQ, D_MODEL)),
        "x_sample": nrm((DEC_BATCH, DEC_SEQ, D_MODEL)),
        "mem_prompt": nrm((BATCH, N_MEM, D_MODEL)),
        "cache_mem_k": nrm((L, DEC_BATCH, N_MEM, X_HEADS, X_HEAD_DIM)),
        "cache_mem_v": nrm((L, DEC_BATCH, N_MEM, X_HEADS, X_HEAD_DIM)),
        "state_pool": nrm((L, DEC_BATCH, POOL_STATE, POOL_WIDTH)),
        "g_mix": gain((L, D_MODEL)),
        "w_in": nrm((L, D_MODEL, D_IN), D_MODEL ** -0.5),
        "g_v": gain((L, SGU_WIDTH)),
        "b_v": nrm((L, SGU_WIDTH), 0.02),
        "w_s": nrm((L, SGU_GROUPS, CHUNK, CHUNK), CHUNK ** -0.5),
        "b_s": gain((L, SGU_GROUPS, CHUNK)),
        "w_pool": nrm((L, POOL_GROUPS, POOL_GROUP_DIM, POOL_GROUP_DIM), POOL_GROUP_DIM ** -0.5),
        "pool_scale": 1.0 + 0.1 * nrm((L, POOL_WIDTH)),
        "g_mem": gain((L, D_MODEL)),
        "w_kv": nrm((L, D_MODEL, 2 * X_WIDTH), D_MODEL ** -0.5),
        "w_out_a": nrm((L, SGU_WIDTH, D_MODEL), SGU_WIDTH ** -0.5),
        "w_out_b": nrm((L, POOL_WIDTH, D_MODEL), POOL_WIDTH ** -0.5),
        "w_out_c": nrm((L, X_WIDTH, D_MODEL), X_WIDTH ** -0.5),
        "w_o": nrm((L, D_MODEL, D_MODEL), D_MODEL ** -0.5),
        "g_ffn": gain((L, D_MODEL)),
        "w_up": nrm((L, D_MODEL, D_FF), D_MODEL ** -0.5),
        "w_down": nrm((L, D_FF, D_MODEL), D_FF ** -0.5),
        "g_final": gain((D_MODEL,)),
    }


def reference(x_prompt, x_sample, mem_prompt, cache_mem_k, cache_mem_v, state_pool,
              g_mix, w_in, g_v, b_v, w_s, b_s, w_pool, pool_scale, g_mem, w_kv,
              w_out_a, w_out_b, w_out_c, w_o, g_ffn, w_up, w_down, g_final):
    yp, ys = x_prompt, x_sample
    zero_prev = jnp.zeros((x_prompt.shape[0], POOL_STATE, POOL_WIDTH), x_prompt.dtype)
    mk_p, mv_p, pool_p, pool_s, cv_p, cv_s = [], [], [], [], [], []
    for l in range(DEPTH):
        lw = (g_mix[l], w_in[l], g_v[l], b_v[l], w_s[l], b_s[l], w_pool[l], pool_scale[l],
              w_out_a[l], w_out_b[l], w_out_c[l], w_o[l], g_ffn[l], w_up[l], w_down[l])
        mem_k, mem_v = memory_kv(mem_prompt, g_mem[l], w_kv[l])
        yp, vrows_p, ptail_p = decoder_layer(yp, zero_prev, 0, mem_k, mem_v, *lw)
        ys, vrows_s, ptail_s = decoder_layer(ys, state_pool[l], PAST_LEN,
                                             cache_mem_k[l], cache_mem_v[l], *lw)
        mk_p.append(mem_k)
        mv_p.append(mem_v)
        pool_p.append(ptail_p)
        pool_s.append(ptail_s)
        cv_p.append(vrows_p)
        cv_s.append(vrows_s)
    y_prompt = rmsnorm(yp, g_final)
    y_sample = rmsnorm(ys, g_final)
    return (y_prompt, y_sample, jnp.stack(mk_p), jnp.stack(mv_p), jnp.stack(pool_p),
            jnp.stack(pool_s), jnp.stack(cv_p), jnp.stack(cv_s))
```

```python
import contextlib
import numpy as np
import concourse.bass as bass
import concourse.mybir as mybir
from concourse.bass_utils import run_bass_kernel_spmd

F32 = mybir.dt.float32
BF16 = mybir.dt.bfloat16
AF = mybir.ActivationFunctionType
ALU = mybir.AluOpType

ENGS = ("pe", "act", "dve", "pool", "sp")
EPS = 1e-6
NSLOT = 5
NF512 = 12
NF1024 = 4


class Buf:
    __slots__ = ("name", "last_w", "readers", "sem", "cnt")

    def __init__(self, name):
        self.name = name
        self.last_w = None
        self.readers = []
        self.sem = None
        self.cnt = 0


class _Op:
    __slots__ = ("eng", "fn", "deps", "signal", "sigcount", "dma", "idx")


class Sched:
    def __init__(self, nc):
        self.nc = nc
        self.ops = {e: [] for e in ENGS}
        self.owners = []

    def buf(self, name):
        return Buf(name)

    @staticmethod
    def _flat(x):
        out = []
        for b in x:
            if isinstance(b, (list, tuple)):
                out.extend(Sched._flat(b))
            else:
                out.append(b)
        return out

    def _register(self, eng, fn, reads, writes, dma):
        reads = self._flat(reads)
        writes = self._flat(writes)
        op = _Op()
        op.eng, op.fn, op.dma = eng, fn, dma
        op.signal = False
        op.idx = len(self.ops[eng])
        deps = []
        for b in list(reads) + list(writes):
            if b.last_w is not None:
                deps.append(b.last_w)
        for b in writes:
            deps.extend(b.readers)
        op.deps = deps
        if dma is not None:
            owner, n = dma
            if owner not in self.owners:
                self.owners.append(owner)
            owner.cnt += 16 * n
            tok = ("dma", owner, owner.cnt)
        else:
            tok = ("eng", op)
        for b in reads:
            b.readers.append(tok)
        for b in writes:
            b.last_w = tok
            b.readers = []
        self.ops[eng].append(op)
        return op

    def op(self, eng, fn, reads=(), writes=()):
        return self._register(eng, fn, reads, writes, None)

    def dma(self, queue, pairs, reads=(), writes=(), owner=None):
        assert owner is not None
        return self._register(queue, pairs, reads, writes, (owner, len(pairs)))

    def emit(self, final_wait_eng="sp"):
        nc = self.nc
        for e in ENGS:
            for op in self.ops[e]:
                for d in op.deps:
                    if d[0] == "eng":
                        p = d[1]
                        if p.eng == "pe" and op.eng == "pe" and op.dma is None:
                            continue
                        p.signal = True
        for e in ENGS:
            c = 0
            for op in self.ops[e]:
                if op.dma is None and op.signal:
                    c += 1
                    op.sigcount = c
        with contextlib.ExitStack() as st:
            esem = {e: st.enter_context(nc.semaphore("s_" + e)) for e in ENGS}
            for i, o in enumerate(self.owners):
                o.sem = st.enter_context(nc.semaphore("d%d" % i))
            block = st.enter_context(nc.Block())
            owners = self.owners

            def run(e, eng):
                seen = {}
                for op in self.ops[e]:
                    waits = {}
                    for d in op.deps:
                        if d[0] == "eng":
                            p = d[1]
                            if p.eng == "pe" and e == "pe" and op.dma is None:
                                continue
                            key, val = esem[p.eng], p.sigcount
                        else:
                            key, val = d[1].sem, d[2]
                        if val > waits.get(key, 0):
                            waits[key] = val
                    for key, val in waits.items():
                        if val > seen.get(key, 0):
                            eng.wait_ge(key, val)
                            seen[key] = val
                    if op.dma is not None:
                        owner, n = op.dma
                        for (o_ap, i_ap) in op.fn:
                            eng.dma_start(out=o_ap, in_=i_ap).then_inc(owner.sem, 16)
                    else:
                        ins = op.fn(eng)
                        if op.signal:
                            ins.then_inc(esem[e], 1)
                if e == final_wait_eng:
                    for o in owners:
                        if o.cnt > seen.get(o.sem, 0):
                            eng.wait_ge(o.sem, o.cnt)

            @block.tensor
            def _(eng):
                run("pe", eng)

            @block.scalar
            def _(eng):
                run("act", eng)

            @block.vector
            def _(eng):
                run("dve", eng)

            @block.gpsimd
            def _(eng):
                run("pool", eng)

            @block.sync
            def _(eng):
                run("sp", eng)


def build_program():
    nc = bass.Bass("TRN2", target_bir_lowering=False)

    def din(name, shape, dt=F32):
        return nc.dram_tensor(name, list(shape), dt, kind="ExternalInput").ap()

    def dout(name, shape):
        return nc.dram_tensor(name, list(shape), F32, kind="ExternalOutput").ap()

    def dscr(name, shape):
        return nc.dram_tensor(name, list(shape), BF16, kind="Internal").ap()

    xp = din("xp", [2048, 1024])
    xs = din("xs", [128, 1024])
    mem = din("mem", [256, 1024])
    ck = din("ck", [16, 256, 256])
    cv = din("cv", [16, 256, 256])
    spst = din("spst", [16, 15, 256])
    g_mix = din("g_mix", [1024])
    w_in = din("w_in", [1024, 4608])
    g_v = din("g_v", [512])
    b_v = din("b_v", [512])
    w_s = din("w_s", [4, 128, 128])
    b_s = din("b_s", [4, 128])
    w_pool = din("w_pool", [4, 64, 64])
    pool_scale = din("pool_scale", [256])
    g_mem = din("g_mem", [1024])
    w_kv = din("w_kv", [1024, 512])
    w_out_a = din("w_out_a", [512, 1024])
    w_out_b = din("w_out_b", [256, 1024])
    w_out_c = din("w_out_c", [256, 1024])
    w_o = din("w_o", [1024, 1024])
    g_ffn = din("g_ffn", [1024])
    w_up = din("w_up", [1024, 4096])
    w_down = din("w_down", [4096, 1024])
    g_final = din("g_final", [1024])

    yp = dout("yp", [2048, 1024])
    ys = dout("ys", [128, 1024])
    mk = dout("mk", [256, 256])
    mv = dout("mv", [256, 256])
    pool_p = dout("pool_p", [15, 256])
    pool_s = dout("pool_s", [16, 15, 256])
    cv_p = dout("cv_p", [128, 512])
    cv_s = dout("cv_s", [128, 512])

    s_all = dscr("s_all", [29, 128, 4096])

    st = contextlib.ExitStack()
    with st, nc.allow_non_contiguous_dma(reason="small strided constant loads"), \
            nc.allow_low_precision(reason="bf16 matmul operands, fp32 accumulate"):
        s = Sched(nc)

        def sb(name, shape, dt):
            return st.enter_context(nc.sbuf_tensor(name, list(shape), dt))

        xres = [sb("xres%d" % t, [128, 1024], F32) for t in range(4)]
        xres_b = [s.buf("xres%d" % t) for t in range(4)]
        hT = sb("hT", [128, 8, 512], BF16)
        hT_bb = [[s.buf("hT%d_%d" % (t, h)) for h in range(2)] for t in range(4)]
        pT = sb("pT", [128, 2, 528], F32)
        pT_b = s.buf("pT")
        pTs = sb("pTs", [128, 2, 368], F32)
        pTs_b = s.buf("pTs")
        arena = sb("arena", [128, 16384], BF16)
        regB = [s.buf("reg%d" % i) for i in range(32)]

        def aview(r0, nreg, k):
            return arena[:, r0 * 512:(r0 + nreg) * 512].rearrange("p (k c) -> p k c", k=k)

        CTX = {
            "prompt": dict(
                merged=aview(0, 8, 8), merged_b=[regB[m] for m in range(8)],
                probs=aview(8, 8, 8), probs_b=[[regB[8 + 4 * hp + i] for i in range(4)] for hp in range(2)],
                apre=aview(16, 4, 4), apre_b=[regB[16 + g] for g in range(4)],
                vh16=aview(20, 4, 4), vh16_b=[regB[20 + t] for t in range(4)],
                pooled=aview(24, 2, 2), pooled_b=[regB[24], regB[25]],
                pm=aview(26, 2, 2), pm_b=[regB[26], regB[27]],
                qT=aview(28, 2, 2), qT_b=[regB[28], regB[29]],
                oT=aview(30, 2, 2), oT_b=[regB[30], regB[31]]),
            "sample": dict(
                merged=sb("merged_s", [128, 8, 128], BF16), merged_b=[s.buf("merged_s%d" % m) for m in range(8)],
                probs=sb("probs_s", [128, 8, 128], BF16), probs_b=[s.buf("probs_s%d" % i) for i in range(2)],
                apre=sb("apre_s", [128, 4, 128], BF16), apre_b=[s.buf("apre_s%d" % g) for g in range(4)],
                vh16=sb("vh16_s", [128, 1, 512], BF16), vh16_b=[s.buf("vh16_s")],
                pooled=sb("pooled_s", [128, 2, 128], BF16), pooled_b=s.buf("pooled_s"),
                pm=sb("pm_s", [128, 2, 128], BF16), pm_b=s.buf("pm_s"),
                qT=sb("qT_s", [128, 2, 128], BF16), qT_b=s.buf("qT_s"),
                oT=sb("oT_s", [128, 2, 128], BF16), oT_b=[s.buf("oT_s%d" % i) for i in range(2)]),
        }
        merged = CTX["prompt"]["merged"]
        merged_b = CTX["prompt"]["merged_b"]
        R = sb("R", [128, 32 * 512], BF16)
        R_b = [s.buf("R%d" % i) for i in range(32)]
        junk = sb("junk", [128, 1024], BF16)
        stats = sb("stats", [128, 512], F32)
        stats_init = s.buf("stats_init")
        slots = [sb("slot%d" % i, [128, 4096], BF16) for i in range(NSLOT)]
        slot_b = [s.buf("slot%d" % i) for i in range(NSLOT)]
        f512 = [sb("f512_%d" % i, [128, 512], F32) for i in range(NF512)]
        f512_b = [s.buf("f512_%d" % i) for i in range(NF512)]
        f1024 = [sb("f1024_%d" % i, [128, 1024], F32) for i in range(NF1024)]
        f1024_b = [s.buf("f1024_%d" % i) for i in range(NF1024)]
        ident = sb("ident", [128, 128], F32)
        maskT = sb("maskT", [128, 128], F32)
        wsT_p = sb("wsT_p", [128, 4, 128], BF16)
        wsT_s = sb("wsT_s", [128, 4, 128], BF16)
        bs_p = sb("bs_p", [128, 4, 128], F32)
        bs_s = sb("bs_s", [128, 4, 128], F32)
        gvb = sb("gvb", [128, 512], F32)
        bvb = sb("bvb", [128, 512], F32)
        gfb = sb("gfb", [128, 1024], F32)
        gcols = sb("gcols", [128, 3, 8], F32)
        pscol = sb("pscol", [128, 2], F32)
        invw = sb("invw", [128, 2], F32)
        wcol = sb("wcol", [128, 2], F32)
        invcnt = sb("invcnt", [128, 2, 16], F32)
        wpbd = sb("wpbd", [128, 2, 128], BF16)
        kT_p = sb("kT_p", [128, 2, 256], BF16)
        v_p = sb("v_p", [128, 2, 256], BF16)
        ones = sb("ones", [128, 64], BF16)
        cst_b = s.buf("consts")
        cst2_b = s.buf("consts2")

        banks = [st.enter_context(nc.psum_tensor("bank%d" % i, [128, 512], F32)) for i in range(8)]
        bank_b = [s.buf("bank%d" % i) for i in range(8)]

        rr = {"bank": 0, "f512": 0, "f1024": 0, "stat": 0, "alt": 0}

        def pb():
            i = rr["bank"]
            rr["bank"] = (i + 1) % 8
            return banks[i], bank_b[i]

        def t512():
            i = rr["f512"]
            rr["f512"] = (i + 1) % NF512
            return f512[i], f512_b[i]

        def t1024():
            i = rr["f1024"]
            rr["f1024"] = (i + 1) % NF1024
            return f1024[i], f1024_b[i]

        def statcols(n):
            i = rr["stat"]
            rr["stat"] = i + n
            assert rr["stat"] <= 512
            return i

        def alt():
            rr["alt"] ^= 1
            return "act" if rr["alt"] else "dve"

        def MM(out, lhsT, rhs, start, stop, R_, W_):
            s.op("pe", lambda e: e.matmul(out, lhsT=lhsT, rhs=rhs, start=start, stop=stop), R_, W_)

        def TR(out, in_, idn, R_, W_):
            s.op("pe", lambda e: e.transpose(out=out, in_=in_, identity=idn), R_, W_)

        def ACT(out, in_, func, R_, W_, **kw):
            s.op("act", lambda e: e.activation(out=out, in_=in_, func=func, **kw), R_, W_)

        route = {"pool2dve": False}

        def rt(eng):
            return "dve" if (eng == "pool" and route["pool2dve"]) else eng

        def TT(eng, out, in0, in1, op, R_, W_):
            eng = rt(eng)
            s.op(eng, lambda e: e.tensor_tensor(out=out, in0=in0, in1=in1, op=op), R_, W_)

        def TS(eng, out, in0, s1, s2, op0, op1, R_, W_):
            eng = rt(eng)
            if s2 is None:
                s.op(eng, lambda e: e.tensor_scalar(out=out, in0=in0, scalar1=s1, scalar2=None, op0=op0), R_, W_)
            else:
                s.op(eng, lambda e: e.tensor_scalar(out=out, in0=in0, scalar1=s1, scalar2=s2, op0=op0, op1=op1), R_, W_)

        def STT(eng, out, in0, scalar, in1, op0, op1, R_, W_):
            eng = rt(eng)
            s.op(eng, lambda e: e.scalar_tensor_tensor(out=out, in0=in0, scalar=scalar, in1=in1, op0=op0, op1=op1), R_, W_)

        def CP(eng, out, in_, R_, W_):
            eng = rt(eng)
            if eng == "act":
                s.op("act", lambda e: e.activation(out=out, in_=in_, func=AF.Identity), R_, W_)
            else:
                s.op(eng, lambda e: e.tensor_copy(out=out, in_=in_), R_, W_)

        def MEMSET(eng, ap, val, W_):
            eng = rt(eng)
            s.op(eng, lambda e: e.memset(ap, val), (), W_)

        def kview(ap, k):
            return ap.rearrange("(k p) c -> p k c", p=128)

        def chunk_pairs(name, dst):
            d8 = dst.rearrange("p (k c) -> p k c", k=8)
            if name == "wkv":
                return [(d8, kview(w_kv, 8))]
            if name.startswith("win"):
                c = int(name[3:])
                return [(d8, kview(w_in[:, c * 512:(c + 1) * 512], 8))]
            if name.startswith("mrg"):
                m = int(name[3:])
                prs = []
                gv_ = dst[:, 0:3072].rearrange("p (k b c) -> p k b c", k=8, b=3)
                for br in range(3):
                    c0 = 1536 + br * 1024 + m * 128
                    prs.append((gv_[:, :, br, :], kview(w_in[:, c0:c0 + 128], 8)))
                prs.append((dst[:, 3072:3584].rearrange("p (k c) -> p k c", k=4), kview(w_out_a[:, m * 128:(m + 1) * 128], 4)))
                prs.append((dst[:, 3584:3840].rearrange("p (k c) -> p k c", k=2), kview(w_out_b[:, m * 128:(m + 1) * 128], 2)))
                prs.append((dst[:, 3840:4096].rearrange("p (k c) -> p k c", k=2), kview(w_out_c[:, m * 128:(m + 1) * 128], 2)))
                return prs
            if name.startswith("wo"):
                n = int(name[2:])
                return [(d8, kview(w_o[:, n * 512:(n + 1) * 512], 8))]
            if name.startswith("wup"):
                c = int(name[3:])
                return [(d8, kview(w_up[:, c * 512:(c + 1) * 512], 8))]
            kc, n = name[3:].split("_")
            kc, n = int(kc), int(n)
            return [(d8, kview(w_down[kc * 1024:(kc + 1) * 1024, n * 512:(n + 1) * 512], 8))]

        block_chunks = (["win1", "win0", "win2"] + ["mrg%d" % m for m in range(8)] + ["wo0", "wo1"] +
                        ["wup%d" % c for c in range(8)] + ["wdn%d_%d" % (kc, n) for kc in range(4) for n in range(2)])
        stream = block_chunks * 5
        stream.insert(3, "wkv")
        sinfo = []
        _n = 0
        for nm_ in stream:
            if nm_ == "wkv":
                sinfo.append((0, -1))
            else:
                sinfo.append((_n // len(block_chunks), _n % len(block_chunks)))
                _n += 1
        chunk_idx = {nm: i for i, nm in enumerate(block_chunks)}
        scr = {nm: s.buf("scr_" + nm) for nm in block_chunks}
        ws = {"next_load": 0, "pos": 0, "wb": None}
        slot_sw = [s.buf("slot_sw%d" % i) for i in range(NSLOT)]

        def wget(expect, hold=0):
            i = ws["pos"]
            assert stream[i] == expect, (stream[i], expect)
            ws["pos"] = i + 1
            while ws["next_load"] < len(stream) and ws["next_load"] <= i + NSLOT - 1 - hold:
                j = ws["next_load"]
                nm = stream[j]
                sl = j % NSLOT
                blk_i, ci = sinfo[j]
                do_cast = (ci < 0) or (blk_i == 0) or (blk_i == 1 and ci % 2 == 1)
                do_wb = (ci >= 0) and ((blk_i == 0 and ci % 2 == 0) or (blk_i == 1 and ci % 2 == 1))
                if do_cast:
                    s.dma("pool", chunk_pairs(nm, slots[sl][:, :]), writes=[slot_b[sl]], owner=slot_sw[sl])
                    if ws["wb"] is not None:
                        pn, psl = ws["wb"]
                        s.dma("pool", [(s_all[chunk_idx[pn]], slots[psl][:, :])], reads=[slot_b[psl]], writes=[scr[pn]], owner=scr[pn])
                    ws["wb"] = (nm, sl) if do_wb else None
                else:
                    if ws["wb"] is not None:
                        pn, psl = ws["wb"]
                        s.dma("pool", [(s_all[chunk_idx[pn]], slots[psl][:, :])], reads=[slot_b[psl]], writes=[scr[pn]], owner=scr[pn])
                        ws["wb"] = None
                    s.dma("sp", [(slots[sl][:, :], s_all[chunk_idx[nm]])], reads=[scr[nm]], writes=[slot_b[sl]], owner=slot_b[sl])
                ws["next_load"] = j + 1
            sl = i % NSLOT
            return slots[sl], slot_b[sl]

        def pbc(ap1d, n):
            return ap1d.partition_broadcast(128)

        cpairs = [
            (bs_p[:], b_s.partition_broadcast(128)),
            (gvb[:], g_v.partition_broadcast(128)),
            (bvb[:], b_v.partition_broadcast(128)),
            (gfb[:], g_final.partition_broadcast(128)),
        ]
        s.dma("sp", cpairs, writes=[cst_b], owner=cst_b)

        ident_b = s.buf("ident")
        mask_b = s.buf("mask")
        MEMSET("pool", ident[:], 1.0, [ident_b])
        s.op("pool", lambda e: e.affine_select(out=ident[:], in_=ident[:], compare_op=ALU.is_equal, fill=0.0, base=0,
                                               pattern=[[-1, 128]], channel_multiplier=1), [ident_b], [ident_b])
        MEMSET("pool", maskT[:], 1.0, [mask_b])
        s.op("pool", lambda e: e.affine_select(out=maskT[:], in_=maskT[:], compare_op=ALU.is_ge, fill=0.0, base=0,
                                               pattern=[[1, 128]], channel_multiplier=-1), [mask_b], [mask_b])

        def mk_small(e):
            e.memset(stats[:], 0.0)
            e.memset(ones[:], 1.0)
            e.memset(invw[0:64, 0:1], 0.5)
            e.memset(invw[64:128, 0:1], 0.25)
            e.memset(invw[0:64, 1:2], 0.125)
            e.memset(invw[64:128, 1:2], 0.0625)
            e.memset(wcol[0:64, 0:1], 2.0)
            e.memset(wcol[64:128, 0:1], 4.0)
            e.memset(wcol[0:64, 1:2], 8.0)
            e.memset(wcol[64:128, 1:2], 16.0)
            ins = None
            for t in range(16):
                ins = e.memset(invcnt[:, :, t:t + 1], float(t + 1))
            return ins

        s.op("pool", mk_small, (), [cst2_b, stats_init])
        for ft in range(2):
            TS("pool", invcnt[:, ft, :], invcnt[:, ft, :], wcol[:, ft:ft + 1], None, ALU.min, None, [cst2_b], [cst2_b])
        s.op("dve", lambda e: e.reciprocal(out=invcnt[:], in_=invcnt[:]), [cst2_b], [cst2_b])

        gst, gst_b = t512()
        s.dma("sp", [(gst[0:8, 0:128], g_mix.rearrange("(k p) -> k p", p=128)),
                     (gst[0:8, 128:256], g_ffn.rearrange("(k p) -> k p", p=128)),
                     (gst[0:8, 256:384], g_mem.rearrange("(k p) -> k p", p=128)),
                     (gst[0:2, 384:512], pool_scale.rearrange("(k p) -> k p", p=128))], writes=[gst_b], owner=gst_b)
        gcol_b = s.buf("gcol")
        bk, bk_b = pb()
        for i in range(3):
            TR(bk[:, i * 8:(i + 1) * 8], gst[0:8, i * 128:(i + 1) * 128], ident[0:8, 0:8], [gst_b, ident_b], [bk_b])
        TR(bk[:, 24:26], gst[0:2, 384:512], ident[0:2, 0:2], [gst_b, ident_b], [bk_b])
        CP("dve", gcols[:].rearrange("p a b -> p (a b)"), bk[:, 0:24], [bk_b], [gcol_b])
        CP("dve", pscol[:], bk[:, 24:26], [bk_b], [gcol_b])

        wpst, wpst_b = t512()
        MEMSET("pool", wpst[:, 0:256], 0.0, [wpst_b])
        wpv = wpst[:, 0:256].rearrange("p (t c) -> p t c", t=2)
        s.dma("sp", [(wpv[(g % 2) * 64:(g % 2) * 64 + 64, g // 2, (g % 2) * 64:(g % 2) * 64 + 64], w_pool[g]) for g in range(4)],
              writes=[wpst_b], owner=wpst_b)
        wpbd_b = s.buf("wpbd")
        CP("pool", wpbd[:], wpv, [wpst_b], [wpbd_b])

        wsn, wsn_b = t512()
        s.dma("sp", [(wsn[:].rearrange("p (g j) -> p g j", g=4), w_s.rearrange("g i j -> i g j"))], writes=[wsn_b], owner=wsn_b)
        bk, bk_b = pb()
        for g in range(4):
            TR(bk[:, g * 128:(g + 1) * 128], wsn[:, g * 128:(g + 1) * 128], ident[:], [wsn_b, ident_b], [bk_b])
        wsT_b = s.buf("wsT")
        wsTs_b = s.buf("wsTs")
        for g in range(4):
            TT("dve", wsT_p[:, g, :], bk[:, g * 128:(g + 1) * 128], maskT[:], ALU.mult, [bk_b, mask_b], [wsT_b])

        csts_b = s.buf("consts_s")
        wss = bs_s_stage = sb("wss", [128, 512], F32)
        wss_b = s.buf("wss")
        wssv = wss[:].rearrange("p (g i) -> p g i", g=4)

        def sample_consts():
            s.dma("sp", [(wssv[0:8, g, 0:8], w_s[g, 0:8, 0:8].rearrange("i j -> j i")) for g in range(4)],
                  writes=[wss_b], owner=wss_b)

        def sample_consts_ops():
            DBL = (8, 16, 32, 64)
            for w in DBL:
                CP("pool", wssv[0:8, :, w:2 * w], wssv[0:8, :, 0:w], [wss_b], [wss_b])
            Lt, Lt_b = t512()
            CP("pool", Lt[0:8, 0:8], ident[0:8, 0:8], [ident_b], [Lt_b])
            for w in DBL:
                CP("pool", Lt[0:8, w:2 * w], Lt[0:8, 0:w], [Lt_b], [Lt_b])
            Et, Et_b = t512()
            MEMSET("pool", Et[0:16, 0:128], 1.0, [Et_b])
            s.op("pool", lambda e: e.affine_select(out=Et[0:16, 0:128], in_=Et[0:16, 0:128], compare_op=ALU.is_ge, fill=0.0,
                                                   base=0, pattern=[[1, 128]], channel_multiplier=-8), [Et_b], [Et_b])
            s.op("pool", lambda e: e.affine_select(out=Et[0:16, 0:128], in_=Et[0:16, 0:128], compare_op=ALU.is_ge, fill=0.0,
                                                   base=7, pattern=[[-1, 128]], channel_multiplier=8), [Et_b], [Et_b])
            bk, bk_b = pb()
            MM(bk[:, 0:128], Et[0:16, 0:128], Et[0:16, 0:128], True, True, [Et_b], [bk_b])
            M2, M2_b = t512()
            TT("dve", M2[:, 0:128], bk[:, 0:128], maskT[:], ALU.mult, [bk_b, mask_b], [M2_b])
            bk2, bk2_b = pb()
            for g in range(4):
                MM(bk2[:, g * 128:(g + 1) * 128], Lt[0:8, 0:128], wssv[0:8, g, :], True, True, [Lt_b, wss_b], [bk2_b])
            for g in range(4):
                TT("dve", wsT_s[:, g, :], bk2[:, g * 128:(g + 1) * 128], M2[:, 0:128], ALU.mult, [bk2_b, M2_b], [wsTs_b])
            CP("pool", bs_s[:, :, 0:8], bs_p[:, :, 0:8], [cst_b], [csts_b])
            for w in DBL:
                CP("pool", bs_s[:, :, w:2 * w], bs_s[:, :, 0:w], [csts_b], [csts_b])

        def rms_rstd(x_ap, x_bufs, width, power=-0.5):
            c = statcols(4)
            sbuf_ = s.buf("st%d" % c)
            ACT(junk[:, 0:width], x_ap, AF.Square, list(x_bufs) + [stats_init], [sbuf_], accum_out=stats[:, c:c + 1])
            TS("dve", stats[:, c + 1:c + 2], stats[:, c:c + 1], 1.0 / width, EPS, ALU.mult, ALU.add, [sbuf_], [sbuf_])
            ACT(stats[:, c + 2:c + 3], stats[:, c + 1:c + 2], AF.Ln, [sbuf_], [sbuf_])
            ACT(stats[:, c + 3:c + 4], stats[:, c + 2:c + 3], AF.Exp, [sbuf_], [sbuf_], scale=power)
            return stats[:, c + 3:c + 4], sbuf_

        def norm_a(x_ap, x_buf, inplace):
            rstd, rb = rms_rstd(x_ap, [x_buf], 1024)
            if inplace is not None:
                xn, xn_b = inplace
            else:
                xn, xn_b = t1024()
            TS("dve", xn[:], x_ap, rstd, None, ALU.mult, None, [x_buf, rb], [xn_b])
            return xn, xn_b

        def norm_b(xn, xn_b, gidx, dstT, dst_bufs, col0):
            for half in range(2):
                bk, bk_b = pb()
                for kk in range(4):
                    k = half * 4 + kk
                    TR(bk[:, kk * 128:(kk + 1) * 128], xn[:, k * 128:(k + 1) * 128], ident[:], [xn_b, ident_b], [bk_b])
                for kk in range(4):
                    k = half * 4 + kk
                    gc = gcols[:, gidx, k:k + 1]
                    o = dstT[:, k, col0:col0 + 128]
                    i_ = bk[:, kk * 128:(kk + 1) * 128]
                    if half == 0:
                        ACT(o, i_, AF.Identity, [bk_b, gcol_b], dst_bufs[half], scale=gc)
                    else:
                        TS("dve", o, i_, gc, None, ALU.mult, None, [bk_b, gcol_b], dst_bufs[half])

        def norm_T(x_ap, x_buf, gidx, dstT, dst_bufs, col0):
            xn, xn_b = norm_a(x_ap, x_buf, None)
            norm_b(xn, xn_b, gidx, dstT, dst_bufs, col0)

        hmT = merged
        hm_l = list(merged_b)
        kvp_b = s.buf("kvp")
        mem_xn = []

        def mem_prep_early():
            for t in range(2):
                s.dma("sp", [(xres[2 + t][:], mem[t * 128:(t + 1) * 128, :])], writes=[xres_b[2 + t]], owner=xres_b[2 + t])
                mem_xn.append(norm_a(xres[2 + t][:], xres_b[2 + t], (xres[2 + t], xres_b[2 + t])))

        def mem_prep_pe():
            for t in range(2):
                norm_b(mem_xn[t][0], mem_xn[t][1], 2, hmT, [hm_l, hm_l], t * 128)
            wk, wk_b = wget("wkv")
            wkv3 = wk[:, :].rearrange("p (k c) -> p k c", k=8)
            for t in range(2):
                bk, bk_b = pb()
                for k in range(8):
                    MM(bk[:, :], hmT[:, k, t * 128:(t + 1) * 128], wkv3[:, k, :], k == 0, k == 7, hm_l + [wk_b], [bk_b])
                kvf, kvf_b = t512()
                CP("dve", kvf[:], bk[:, :], [bk_b], [kvf_b])
                CP("pool", v_p[:, t, :], kvf[:, 256:512], [kvf_b], [kvp_b])
                s.dma("sp", [(mk[t * 128:(t + 1) * 128, :], kvf[:, 0:256]), (mv[t * 128:(t + 1) * 128, :], kvf[:, 256:512])],
                      reads=[kvf_b], owner=kvf_b)
            for hp in range(2):
                bk, bk_b = pb()
                for k in range(8):
                    MM(bk[:, 0:256], wkv3[:, k, hp * 128:(hp + 1) * 128], hmT[:, k, 0:256], k == 0, k == 7, hm_l + [wk_b], [bk_b])
                CP("dve", kT_p[:, hp, :], bk[:, 0:256], [bk_b], [kvp_b])

        def sample_kv_prep(b):
            t1, t1_b = t512()
            s.dma("sp", [(t1[:].rearrange("p (c f) -> p c f", c=2), ck[b].rearrange("(c m) f -> m c f", m=128))],
                  writes=[t1_b], owner=t1_b)
            bk, bk_b = pb()
            for c in range(2):
                for hp in range(2):
                    TR(bk[:, hp * 256 + c * 128: hp * 256 + (c + 1) * 128], t1[:, c * 256 + hp * 128: c * 256 + (hp + 1) * 128],
                       ident[:], [t1_b, ident_b], [bk_b])
            CP("dve", arena[:, b * 512:(b + 1) * 512], bk[:, :], [bk_b], [regB[b]])
            t2, t2_b = t512()
            s.dma("sp", [(t2[:].rearrange("p (c f) -> p c f", c=2), cv[b].rearrange("(c m) f -> m c f", m=128))],
                  writes=[t2_b], owner=t2_b)
            CP("pool", arena[:, (16 + b) * 512:(17 + b) * 512], t2[:], [t2_b], [regB[16 + b]])

        def pTs3(ft):
            return pTs[:, ft, :].rearrange("p (b c) -> p b c", c=23)

        cpy_b = s.buf("poolcopy")

        def sample_hist_prep():
            for rt_ in range(2):
                tS, tS_b = t512()
                s.dma("sp", [(tS[0:120, 0:256], spst[rt_ * 8:(rt_ + 1) * 8].rearrange("b r c -> (b r) c"))], writes=[tS_b], owner=tS_b)
                bk, bk_b = pb()
                for ft in range(2):
                    TR(bk[:, ft * 128: ft * 128 + 120], tS[0:120, ft * 128:(ft + 1) * 128], ident[0:120, 0:120], [tS_b, ident_b], [bk_b])
                for ft in range(2):
                    CP("dve", pTs3(ft)[:, rt_ * 8:(rt_ + 1) * 8, 0:15],
                       bk[:, ft * 128: ft * 128 + 120].rearrange("p (b r) -> p b r", r=15), [bk_b], [pTs_b])
            s.dma("sp", [(pool_s[:, 0:7, :], spst[:, 8:15, :])], owner=cpy_b)

        def blk_info(kind, j):
            sample = kind == "sample"
            NT = 1 if sample else 4
            return sample, NT

        def P0a(kind, j):
            sample, NT = blk_info(kind, j)
            res = []
            for t in range(NT):
                src = xs if sample else xp[j * 512 + t * 128: j * 512 + (t + 1) * 128, :]
                T_, T_b = t1024()
                s.dma("sp", [(T_[:], src)], writes=[T_b], owner=T_b)
                res.append(norm_a(T_[:], T_b, (T_, T_b)))
            return res

        def P0b(kind, j, xns):
            sample, NT = blk_info(kind, j)
            for t in range(NT):
                norm_b(xns[t][0], xns[t][1], 0, hT, [[hT_bb[t][0]], [hT_bb[t][1]]], t * 128)

        def run_main(kind, j, nxt):
            sample = kind == "sample"
            NT = 1 if sample else 4
            cols = 128 * NT
            hT_all = [hT_bb[t][h] for t in range(NT) for h in range(2)]
            last = (not sample) and j == 3
            Rt, Rt_b = R, R_b
            rw = 128 if sample else 512
            C = CTX[kind]
            merged, merged_b, probs, probs_b = C["merged"], C["merged_b"], C["probs"], C["probs_b"]
            apre, apre_b, vh16, vh16_b = C["apre"], C["apre_b"], C["vh16"], C["vh16_b"]
            pooled, pooled_b, pm, pm_b = C["pooled"], C["pooled_b"], C["pm"], C["pm_b"]
            qT, qT_b, oT, oT_b = C["qT"], C["qT_b"], C["oT"], C["oT_b"]
            pTb_ = pTs_b if sample else pT_b
            route["pool2dve"] = (not sample) and j <= 1

            first_blk = (not sample) and j == 0

            def load_xres(t):
                src = xs if sample else xp[j * 512 + t * 128: j * 512 + (t + 1) * 128, :]
                s.dma("sp", [(xres[t][:], src)], writes=[xres_b[t]], owner=xres_b[t])

            for t in range(NT):
                if not (first_blk and t >= 2):
                    load_xres(t)

            w1, w1_b = wget("win1", hold=(3 if first_blk else 0))
            w13 = w1[:, :].rearrange("p (k c) -> p k c", k=8)
            gvs = []
            c0 = statcols(3 * 4)
            lnb = s.buf("lnstats%d" % c0)
            for t in range(NT):
                bk, bk_b = pb()
                for k in range(8):
                    MM(bk[:, :], hT[:, k, t * 128:(t + 1) * 128], w13[:, k, :], k == 0, k == 7, hT_bb[t] + [w1_b], [bk_b])
                gv, gv_b = t512()
                ACT(gv[:], bk[:, :], AF.Gelu_apprx_tanh, [bk_b], [gv_b])
                gvs.append((gv, gv_b))
            mvt = stats[:, c0:c0 + 8].rearrange("p (t c) -> p t c", c=2)
            c6 = statcols(6 * 4)
            for t in range(NT):
                gv, gv_b = gvs[t]
                st6b = s.buf("st6_%d_%d" % (c6, t))
                s.op("dve", lambda e, o=stats[:, c6 + 6 * t:c6 + 6 * t + 6], i=gv[:]: e.bn_stats(out=o, in_=i), [gv_b, stats_init], [st6b])
                s.op("dve", lambda e, o=mvt[:, t, :], i=stats[:, c6 + 6 * t:c6 + 6 * t + 6]: e.bn_aggr(out=o, in_=i), [st6b, stats_init], [lnb])
            TS("dve", stats[:, c0 + 8:c0 + 8 + NT], mvt[:, 0:NT, 1], EPS, None, ALU.add, None, [lnb], [lnb])
            ACT(stats[:, c0 + 8:c0 + 8 + NT], stats[:, c0 + 8:c0 + 8 + NT], AF.Ln, [lnb], [lnb])
            ACT(stats[:, c0 + 8:c0 + 8 + NT], stats[:, c0 + 8:c0 + 8 + NT], AF.Exp, [lnb], [lnb], scale=-0.5)
            for t in range(NT):
                gv, gv_b = gvs[t]
                eng2 = "dve" if (t % 2 == 0) else "pool"
                TS("dve", gv[:], gv[:], mvt[:, t, 0:1], stats[:, c0 + 8 + t:c0 + 9 + t], ALU.subtract, ALU.mult, [gv_b, lnb], [gv_b])
                TT(eng2, gv[:], gv[:], gvb[:], ALU.mult, [gv_b, cst_b], [gv_b])
                if sample or (last and t == 3):
                    TT(eng2, gv[:], gv[:], bvb[:], ALU.add, [gv_b, cst_b], [gv_b])
                    s.dma("sp", [((cv_s if sample else cv_p)[:, :], gv[:])], reads=[gv_b], owner=gv_b)
                    CP(eng2, vh16[:, t, :], gv[:], [gv_b], [vh16_b[t]])
                else:
                    TT(eng2, vh16[:, t, :], gv[:], bvb[:], ALU.add, [gv_b, cst_b], [vh16_b[t]])

            w0, w0_b = wget("win0")
            w03 = w0[:, :].rearrange("p (k c) -> p k c", k=8)
            ugs = []
            for g in range(4):
                bk, bk_b = pb()
                for k in range(8):
                    MM(bk[:, 0:cols], w03[:, k, g * 128:(g + 1) * 128], hT[:, k, 0:cols], k == 0, k == 7, hT_all + [w0_b], [bk_b])
                ug, ug_b = t512()
                ACT(ug[:, 0:cols], bk[:, 0:cols], AF.Gelu_apprx_tanh, [bk_b], [ug_b])
                ugs.append((ug, ug_b))

            w2, w2_b = wget("win2")
            w23 = w2[:, :].rearrange("p (k c) -> p k c", k=8)
            if not sample:
                if j == 0:
                    MEMSET("pool", pT[:, :, 0:16], 0.0, [pTb_])
                else:
                    CP("pool", pT[:, :, 1:16], pT[:, :, 513:528], [pTb_], [pTb_])
            for ft in range(2):
                bk, bk_b = pb()
                for k in range(8):
                    MM(bk[:, 0:cols], w23[:, k, ft * 128:(ft + 1) * 128], hT[:, k, 0:cols], k == 0, k == 7, hT_all + [w2_b], [bk_b])
                if sample:
                    CP("dve", pTs3(ft)[:, :, 15:23], bk[:, 0:128].rearrange("p (b l) -> p b l", l=8), [bk_b], [pTb_])
                else:
                    CP("dve", pT[:, ft, 16:528], bk[:, :], [bk_b], [pTb_])
            for ft in range(2):
                bk, bk_b = pb()
                for k in range(8):
                    MM(bk[:, 0:cols], w23[:, k, 256 + ft * 128: 256 + (ft + 1) * 128], hT[:, k, 0:cols], k == 0, k == 7,
                       hT_all + [w2_b], [bk_b])
                CP("act", qT[:, ft, 0:cols], bk[:, 0:cols], [bk_b], [qT_b])
            if sample or last:
                tt = 0 if sample else 3
                bk, bk_b = pb()
                for k in range(8):
                    MM(bk[:, 0:256], hT[:, k, tt * 128:(tt + 1) * 128], w23[:, k, 0:256], k == 0, k == 7, hT_bb[tt] + [w2_b], [bk_b])
                ptk, ptk_b = t512()
                CP("act", ptk[:, 0:256], bk[:, 0:256], [bk_b], [ptk_b])
                if sample:
                    s.dma("sp", [(pool_s[b, 7:15, :], ptk[b * 8:(b + 1) * 8, 0:256]) for b in range(16)], reads=[ptk_b], owner=ptk_b)
                else:
                    s.dma("sp", [(pool_p[:, :], ptk[113:128, 0:256])], reads=[ptk_b], owner=ptk_b)

            if first_blk:
                mem_prep_pe()
                load_xres(2)
                load_xres(3)

            if sample:
                def P3(ft):
                    return pTs3(ft)

                def V3(tile):
                    return tile[:, 0:368].rearrange("p (b c) -> p b c", c=23)

                def O3(ft):
                    return pooled[:, ft, 0:128].rearrange("p (b l) -> p b l", l=8)

                A, A_b = t1024()
                Bt, Bt_b = t1024()
                TT("pool", V3(A)[:, :, 1:23], P3(0)[:, :, 1:23], P3(0)[:, :, 0:22], ALU.add, [pTb_], [A_b])
                TT("pool", V3(Bt)[64:128, :, 3:23], V3(A)[64:128, :, 3:23], V3(A)[64:128, :, 1:21], ALU.add, [A_b], [Bt_b])
                STT("dve", O3(0)[0:64], V3(A)[0:64, :, 15:23], invw[0:64, 0:1], P3(0)[0:64, :, 15:23], ALU.mult, ALU.subtract,
                    [A_b, pTb_, cst2_b], [pooled_b])
                STT("dve", O3(0)[64:128], V3(Bt)[64:128, :, 15:23], invw[64:128, 0:1], P3(0)[64:128, :, 15:23], ALU.mult, ALU.subtract,
                    [Bt_b, pTb_, cst2_b], [pooled_b])
                A2, A2_b = t1024()
                B2, B2_b = t1024()
                TT("pool", V3(A2)[:, :, 1:23], P3(1)[:, :, 1:23], P3(1)[:, :, 0:22], ALU.add, [pTb_], [A2_b])
                TT("pool", V3(B2)[:, :, 3:23], V3(A2)[:, :, 3:23], V3(A2)[:, :, 1:21], ALU.add, [A2_b], [B2_b])
                C2, C2_b = t1024()
                TT("pool", V3(C2)[:, :, 7:23], V3(B2)[:, :, 7:23], V3(B2)[:, :, 3:19], ALU.add, [B2_b], [C2_b])
                D2, D2_b = t1024()
                TT("pool", V3(D2)[64:128, :, 15:23], V3(C2)[64:128, :, 15:23], V3(C2)[64:128, :, 7:15], ALU.add, [C2_b], [D2_b])
                STT("dve", O3(1)[0:64], V3(C2)[0:64, :, 15:23], invw[0:64, 1:2], P3(1)[0:64, :, 15:23], ALU.mult, ALU.subtract,
                    [C2_b, pTb_, cst2_b], [pooled_b])
                STT("dve", O3(1)[64:128], V3(D2)[64:128, :, 15:23], invw[64:128, 1:2], P3(1)[64:128, :, 15:23], ALU.mult, ALU.subtract,
                    [D2_b, pTb_, cst2_b], [pooled_b])
            else:
                fxs = []

                def fin(ft, lo, hi, S_, S_b):
                    STT("dve", pooled[lo:hi, ft, :], S_[lo:hi, 16:528], invw[lo:hi, ft:ft + 1], pT[lo:hi, ft, 16:528],
                        ALU.mult, ALU.subtract, [S_b, pTb_, cst2_b], [pooled_b])
                    if j == 0:
                        if not fxs:
                            fxs.append(t512())
                        fx, fx_b = fxs[0]
                        c_ = ft * 16
                        TT("dve", fx[lo:hi, c_:c_ + 15], S_[lo:hi, 16:31], invcnt[lo:hi, ft, 0:15], ALU.mult, [S_b, cst2_b], [fx_b])
                        TT("dve", pooled[lo:hi, ft, 0:15], fx[lo:hi, c_:c_ + 15], pT[lo:hi, ft, 16:31], ALU.subtract, [fx_b, pTb_], [pooled_b])

                A, A_b = t1024()
                Bt, Bt_b = t1024()
                TT("pool", A[:, 2:528], pT[:, 0, 2:528], pT[:, 0, 1:527], ALU.add, [pTb_], [A_b])
                TT("pool", Bt[64:128, 4:528], A[64:128, 4:528], A[64:128, 2:526], ALU.add, [A_b], [Bt_b])
                fin(0, 0, 64, A, A_b)
                fin(0, 64, 128, Bt, Bt_b)
                A2, A2_b = t1024()
                B2, B2_b = t1024()
                TT("dve", A2[:, 2:528], pT[:, 1, 2:528], pT[:, 1, 1:527], ALU.add, [pTb_], [A2_b])
                TT("dve", B2[:, 4:528], A2[:, 4:528], A2[:, 2:526], ALU.add, [A2_b], [B2_b])
                C2, C2_b = t1024()
                TT("dve", C2[:, 8:528], B2[:, 8:528], B2[:, 4:524], ALU.add, [B2_b], [C2_b])
                D2, D2_b = t1024()
                TT("dve", D2[64:128, 16:528], C2[64:128, 16:528], C2[64:128, 8:520], ALU.add, [C2_b], [D2_b])
                fin(1, 0, 64, C2, C2_b)
                fin(1, 64, 128, D2, D2_b)
            for hp in range(2):
                pr_b = probs_b[hp]
                if sample:
                    for hh in range(2):
                        bk, bk_b = pb()
                        for b in range(16):
                            kTb = arena[:, b * 512:(b + 1) * 512].rearrange("p (h m) -> p h m", h=2)
                            for mc in range(2):
                                cc = mc * 128 + b * 8
                                MM(bk[:, cc:cc + 8], kTb[hh * 64:(hh + 1) * 64, hp, mc * 128:(mc + 1) * 128],
                                   qT[hh * 64:(hh + 1) * 64, hp, b * 8:(b + 1) * 8], True, True, [regB[b], qT_b], [bk_b])
                        ACT(probs[:, hp * 4 + hh * 2:hp * 4 + hh * 2 + 2, 0:128], bk[:, 0:256].rearrange("p (i c) -> p i c", i=2), AF.Exp,
                            [bk_b], [pr_b], scale=0.125)
                else:
                    for hh in range(2):
                        for mc in range(2):
                            bk, bk_b = pb()
                            MM(bk[:, 0:cols], kT_p[hh * 64:(hh + 1) * 64, hp, mc * 128:(mc + 1) * 128],
                               qT[hh * 64:(hh + 1) * 64, hp, 0:cols], True, True, [kvp_b, qT_b], [bk_b])
                            ACT(probs[:, hp * 4 + hh * 2 + mc, 0:cols], bk[:, 0:cols], AF.Exp, [bk_b], [pr_b], scale=0.125)
            def pool_mix():
                for ft in range(2):
                    bk, bk_b = pb()
                    MM(bk[:, 0:cols], wpbd[:, ft, :], pooled[:, ft, 0:cols], True, True, [pooled_b, wpbd_b], [bk_b])
                    ACT(pm[:, ft, 0:cols], bk[:, 0:cols], AF.Identity, [bk_b, gcol_b], [pm_b], scale=pscol[:, ft:ft + 1])

            for hp in range(2):
                pr_b = probs_b[hp]
                bo, bo_b = pb()
                bd, bd_b = pb()
                if sample:
                    for b in range(16):
                        vb = arena[:, (16 + b) * 512:(17 + b) * 512].rearrange("p (c f) -> p c f", c=2)
                        for hh in range(2):
                            h = 2 * hp + hh
                            for mc in range(2):
                                MM(bo[hh * 64:(hh + 1) * 64, b * 8:(b + 1) * 8], vb[:, mc, h * 64:(h + 1) * 64],
                                   probs[:, hp * 4 + hh * 2 + mc, b * 8:(b + 1) * 8], mc == 0, mc == 1, [regB[16 + b], pr_b], [bo_b])
                    for hh in range(2):
                        for mc in range(2):
                            MM(bd[hh * 64:(hh + 1) * 64, 0:128], ones[:, 0:64], probs[:, hp * 4 + hh * 2 + mc, 0:128],
                               mc == 0, mc == 1, [pr_b, cst2_b], [bd_b])
                else:
                    for hh in range(2):
                        h = 2 * hp + hh
                        for mc in range(2):
                            MM(bo[hh * 64:(hh + 1) * 64, 0:cols], v_p[:, mc, h * 64:(h + 1) * 64],
                               probs[:, hp * 4 + hh * 2 + mc, 0:cols], mc == 0, mc == 1, [kvp_b, pr_b], [bo_b])
                    for hh in range(2):
                        for mc in range(2):
                            MM(bd[hh * 64:(hh + 1) * 64, 0:cols], ones[:, 0:64], probs[:, hp * 4 + hh * 2 + mc, 0:cols],
                               mc == 0, mc == 1, [pr_b, cst2_b], [bd_b])
                rd, rd_b = t512()
                ACT(rd[:, 0:cols], bd[:, 0:cols], AF.Ln, [bd_b], [rd_b])
                ACT(rd[:, 0:cols], rd[:, 0:cols], AF.Exp, [rd_b], [rd_b], scale=-1.0)
                TT("dve", oT[:, hp, 0:cols], bo[:, 0:cols], rd[:, 0:cols], ALU.mult, [bo_b, rd_b], [oT_b[hp]])

            wsT = wsT_s if sample else wsT_p
            bsx = bs_s if sample else bs_p
            wsTb_ = wsTs_b if sample else wsT_b
            bsb_ = csts_b if sample else cst_b
            for g in range(4):
                ug, ug_b = ugs[g]
                bk, bk_b = pb()
                for t in range(NT):
                    MM(bk[:, t * 128:(t + 1) * 128], vh16[:, t, g * 128:(g + 1) * 128], wsT[:, g, :], True, True,
                       [vh16_b[t], wsTb_], [bk_b])
                tm, tm_b = t512()
                for t in range(NT):
                    TT("dve", tm[:, t * 128:(t + 1) * 128], bk[:, t * 128:(t + 1) * 128], bsx[:, g, :], ALU.add, [bk_b, bsb_], [tm_b])
                TT("pool", apre[:, g, 0:cols], tm[:, 0:cols], ug[:, 0:cols], ALU.mult, [tm_b, ug_b], [apre_b[g]])

            for m in range(8):
                wm, wm_b = wget("mrg%d" % m)
                gb = []
                for br in range(3):
                    bk, bk_b = pb()
                    for k in range(8):
                        MM(bk[:, 0:cols], wm[:, (k * 3 + br) * 128:(k * 3 + br + 1) * 128], hT[:, k, 0:cols], k == 0, k == 7,
                           hT_all + [wm_b], [bk_b])
                    gb.append((bk, bk_b))
                if m == 0:
                    pool_mix()
                ob = []
                bk, bk_b = pb()
                for k in range(4):
                    MM(bk[:, 0:cols], wm[:, 3072 + k * 128: 3072 + (k + 1) * 128], apre[:, k, 0:cols], k == 0, k == 3,
                       [apre_b[k], wm_b], [bk_b])
                ob.append((bk, bk_b))
                bk, bk_b = pb()
                for k in range(2):
                    MM(bk[:, 0:cols], wm[:, 3584 + k * 128: 3584 + (k + 1) * 128], pm[:, k, 0:cols], k == 0, k == 1, [pm_b, wm_b], [bk_b])
                ob.append((bk, bk_b))
                bk, bk_b = pb()
                for k in range(2):
                    MM(bk[:, 0:cols], wm[:, 3840 + k * 128: 3840 + (k + 1) * 128], oT[:, k, 0:cols], k == 0, k == 1, [oT_b[k], wm_b], [bk_b])
                ob.append((bk, bk_b))
                ts_ = []
                for br in range(3):
                    sg, sg_b = t512()
                    ACT(sg[:, 0:cols], gb[br][0][:, 0:cols], AF.Sigmoid, [gb[br][1]], [sg_b])
                    TT("dve", sg[:, 0:cols], ob[br][0][:, 0:cols], sg[:, 0:cols], ALU.mult, [ob[br][1], sg_b], [sg_b])
                    ts_.append((sg, sg_b))
                TT("pool", ts_[0][0][:, 0:cols], ts_[0][0][:, 0:cols], ts_[1][0][:, 0:cols], ALU.add, [ts_[0][1], ts_[1][1]], [ts_[0][1]])
                TT("pool", merged[:, m, 0:cols], ts_[0][0][:, 0:cols], ts_[2][0][:, 0:cols], ALU.add, [ts_[0][1], ts_[2][1]], [merged_b[m]])

            wo_l = []
            for n in range(2):
                wo_, wo_b = wget("wo%d" % n, hold=n)
                wo_l.append((wo_[:, :].rearrange("p (k c) -> p k c", k=8), wo_b))
            r2s = []
            for t in range(NT):
                for n in range(2):
                    wo3, wo_b = wo_l[n]
                    bk, bk_b = pb()
                    for k in range(8):
                        MM(bk[:, :], merged[:, k, t * 128:(t + 1) * 128], wo3[:, k, :], k == 0, k == 7, [merged_b[k], wo_b], [bk_b])
                    TT("dve", xres[t][:, n * 512:(n + 1) * 512], bk[:, :], xres[t][:, n * 512:(n + 1) * 512], ALU.add,
                       [bk_b, xres_b[t]], [xres_b[t]])
                r2s.append(rms_rstd(xres[t][:], [xres_b[t]], 1024, power=-1.0))
                if t >= 1:
                    norm_b(xres[t - 1], xres_b[t - 1], 1, hT, [[hT_bb[t - 1][0]], [hT_bb[t - 1][1]]], (t - 1) * 128)
            def ffn_evac(f, bk, bk_b):
                rl, rl_b = t512()
                ACT(rl[:, 0:cols], bk[:, 0:cols], AF.Relu, [bk_b], [rl_b])
                TT("pool" if (f % 2) else "dve", Rt[:, f * rw: f * rw + cols], rl[:, 0:cols], rl[:, 0:cols], ALU.mult, [rl_b], [Rt_b[f]])

            def sample_hooks(c):
                if nxt is not None and nxt[0] == "sample":
                    if c == 0:
                        sample_hist_prep()
                        sample_consts_ops()
                    sample_kv_prep(2 * c)
                    sample_kv_prep(2 * c + 1)

            split = NT == 4
            if split:
                sample_hooks(0)
                wu, wu_b = wget("wup0")
                wu3 = wu[:, :].rearrange("p (k c) -> p k c", k=8)
                bk0 = []
                for mt in range(4):
                    bk, bk_b = pb()
                    bk0.append((bk, bk_b))
                    for k in range(8):
                        MM(bk[:, 0:384], wu3[:, k, mt * 128:(mt + 1) * 128], hT[:, k, 0:384], k == 0, k == 7,
                           hT_all[0:6] + [wu_b], [bk_b])
            norm_b(xres[NT - 1], xres_b[NT - 1], 1, hT, [[hT_bb[NT - 1][0]], [hT_bb[NT - 1][1]]], (NT - 1) * 128)
            if split:
                for mt in range(4):
                    bk, bk_b = bk0[mt]
                    for k in range(8):
                        MM(bk[:, 384:512], wu3[:, k, mt * 128:(mt + 1) * 128], hT[:, k, 384:512], k == 0, k == 7,
                           hT_bb[3] + [wu_b], [bk_b])
                    ffn_evac(mt, bk, bk_b)

            nxt_xn = P0a(*nxt) if nxt is not None else None

            for c in range(8):
                if split and c == 0:
                    continue
                sample_hooks(c)
                wu, wu_b = wget("wup%d" % c)
                wu3 = wu[:, :].rearrange("p (k c) -> p k c", k=8)
                for mt in range(4):
                    f = c * 4 + mt
                    bk, bk_b = pb()
                    for k in range(8):
                        MM(bk[:, 0:cols], wu3[:, k, mt * 128:(mt + 1) * 128], hT[:, k, 0:cols], k == 0, k == 7, hT_all + [wu_b], [bk_b])
                    ffn_evac(f, bk, bk_b)

            if nxt is not None:
                P0b(nxt[0], nxt[1], nxt_xn)

            for kc in range(4):
                for n in range(2):
                    wd, wd_b = wget("wdn%d_%d" % (kc, n))
                    wd3 = wd[:, :].rearrange("p (k c) -> p k c", k=8)
                    for t in range(NT):
                        bi = 2 * t + n
                        for k in range(8):
                            f = kc * 8 + k
                            MM(banks[bi][:, :], Rt[:, f * rw + t * 128: f * rw + (t + 1) * 128], wd3[:, k, :],
                               kc == 0 and k == 0, kc == 3 and k == 7, [Rt_b[f], wd_b], [bank_b[bi]])
            rr["bank"] = 0
            for t in range(NT):
                for n in range(2):
                    bi = 2 * t + n
                    STT("dve", xres[t][:, n * 512:(n + 1) * 512], banks[bi][:, :], r2s[t][0], xres[t][:, n * 512:(n + 1) * 512],
                        ALU.mult, ALU.add, [bank_b[bi], xres_b[t], r2s[t][1]], [xres_b[t]])
                rstd, rb = rms_rstd(xres[t][:], [xres_b[t]], 1024)
                yt, yt_b = t1024()
                STT("dve", yt[:], xres[t][:], rstd, gfb[:], ALU.mult, ALU.mult, [xres_b[t], rb, cst_b], [yt_b])
                dst = ys if sample else yp[j * 512 + t * 128: j * 512 + (t + 1) * 128, :]
                s.dma("sp", [(dst, yt[:])], reads=[yt_b], owner=yt_b)
            route["pool2dve"] = False

        order = [("prompt", j) for j in range(4)] + [("sample", 0)]
        first_xn = P0a(*order[0])
        mem_prep_early()
        P0b(order[0][0], order[0][1], first_xn)
        for i, (kind, j) in enumerate(order):
            run_main(kind, j, order[i + 1] if i + 1 < len(order) else None)
            if i == 0:
                sample_consts()
        assert ws["pos"] == len(stream)
        s.emit()
    return nc


_NC = None


def kernel(x_prompt, x_sample, mem_prompt, cache_mem_k, cache_mem_v, state_pool,
           g_mix, w_in, g_v, b_v, w_s, b_s, w_pool, pool_scale, g_mem, w_kv,
           w_out_a, w_out_b, w_out_c, w_o, g_ffn, w_up, w_down, g_final):
    global _NC
    if _NC is None:
        _NC = build_program()
    nc = _NC
    f = lambda a: np.ascontiguousarray(np.asarray(a, dtype=np.float32))
    shared = {
        "g_mix": f(g_mix[0]), "w_in": f(w_in[0]), "g_v": f(g_v[0]), "b_v": f(b_v[0]), "w_s": f(w_s[0]),
        "b_s": f(b_s[0]), "w_pool": f(w_pool[0]), "pool_scale": f(pool_scale[0]), "g_mem": f(g_mem[0]),
        "w_kv": f(w_kv[0]), "w_out_a": f(w_out_a[0]), "w_out_b": f(w_out_b[0]), "w_out_c": f(w_out_c[0]),
        "w_o": f(w_o[0]), "g_ffn": f(g_ffn[0]), "w_up": f(w_up[0]), "w_down": f(w_down[0]), "g_final": f(g_final),
    }
    in_maps = []
    for c in range(8):
        d = dict(shared)
        d["xp"] = f(x_prompt[c])
        d["xs"] = f(np.asarray(x_sample)[16 * c:16 * (c + 1)].reshape(128, 1024))
        d["mem"] = f(mem_prompt[c])
        d["ck"] = f(np.asarray(cache_mem_k)[0, 16 * c:16 * (c + 1)].reshape(16, 256, 256))
        d["cv"] = f(np.asarray(cache_mem_v)[0, 16 * c:16 * (c + 1)].reshape(16, 256, 256))
        d["spst"] = f(np.asarray(state_pool)[0, 16 * c:16 * (c + 1)])
        in_maps.append(d)
    res = run_bass_kernel_spmd(nc, in_maps, core_ids=list(range(8)))
    r = res.results
    y_prompt = np.stack([r[c]["yp"] for c in range(8)], 0).astype(np.float32)
    y_sample = np.concatenate([r[c]["ys"].reshape(16, 8, 1024) for c in range(8)], 0).astype(np.float32)
    mem_k = np.stack([r[c]["mk"].reshape(256, 4, 64) for c in range(8)], 0)[None].astype(np.float32)
    mem_v = np.stack([r[c]["mv"].reshape(256, 4, 64) for c in range(8)], 0)[None].astype(np.float32)
    pool_p = np.stack([r[c]["pool_p"] for c in range(8)], 0)[None].astype(np.float32)
    pool_s = np.concatenate([r[c]["pool_s"] for c in range(8)], 0)[None].astype(np.float32)
    cv_p = np.stack([r[c]["cv_p"] for c in range(8)], 0)[None].astype(np.float32)
    cv_s = np.concatenate([r[c]["cv_s"].reshape(16, 8, 512) for c in range(8)], 0)[None].astype(np.float32)
    return (y_prompt, y_sample, mem_k, mem_v, pool_p, pool_s, cv_p, cv_s)
```

```python
import contextlib
import numpy as np
import concourse.bass as bass
import concourse.mybir as mybir
from concourse.bass_utils import run_bass_kernel_spmd

F32 = mybir.dt.float32
BF16 = mybir.dt.bfloat16
AF = mybir.ActivationFunctionType
ALU = mybir.AluOpType

ENGS = ("pe", "act", "dve", "pool", "sp")
EPS = 1e-6
NSLOT = 5
NF512 = 12
NF1024 = 4


class Buf:
    __slots__ = ("name", "last_w", "readers", "sem", "cnt")

    def __init__(self, name):
        self.name = name
        self.last_w = None
        self.readers = []
        self.sem = None
        self.cnt = 0


class _Op:
    __slots__ = ("eng", "fn", "deps", "signal", "sigcount", "dma", "idx")


class Sched:
    def __init__(self, nc):
        self.nc = nc
        self.ops = {e: [] for e in ENGS}
        self.owners = []

    def buf(self, name):
        return Buf(name)

    @staticmethod
    def _flat(x):
        out = []
        for b in x:
            if isinstance(b, (list, tuple)):
                out.extend(Sched._flat(b))
            else:
                out.append(b)
        return out

    def _register(self, eng, fn, reads, writes, dma):
        reads = self._flat(reads)
        writes = self._flat(writes)
        op = _Op()
        op.eng, op.fn, op.dma = eng, fn, dma
        op.signal = False
        op.idx = len(self.ops[eng])
        deps = []
        for b in list(reads) + list(writes):
            if b.last_w is not None:
                deps.append(b.last_w)
        for b in writes:
            deps.extend(b.readers)
        op.deps = deps
        if dma is not None:
            owner, n = dma
            if owner not in self.owners:
                self.owners.append(owner)
            owner.cnt += 16 * n
            tok = ("dma", owner, owner.cnt)
        else:
            tok = ("eng", op)
        for b in reads:
            b.readers.append(tok)
        for b in writes:
            b.last_w = tok
            b.readers = []
        self.ops[eng].append(op)
        return op

    def op(self, eng, fn, reads=(), writes=()):
        return self._register(eng, fn, reads, writes, None)

    def dma(self, queue, pairs, reads=(), writes=(), owner=None):
        assert owner is not None
        return self._register(queue, pairs, reads, writes, (owner, len(pairs)))

    def emit(self, final_wait_eng="sp"):
        nc = self.nc
        for e in ENGS:
            for op in self.ops[e]:
                for d in op.deps:
                    if d[0] == "eng":
                        p = d[1]
                        if p.eng == "pe" and op.eng == "pe" and op.dma is None:
                            continue
                        p.signal = True
        for e in ENGS:
            c = 0
            for op in self.ops[e]:
                if op.dma is None and op.signal:
                    c += 1
                    op.sigcount = c
        with contextlib.ExitStack() as st:
            esem = {e: st.enter_context(nc.semaphore("s_" + e)) for e in ENGS}
            for i, o in enumerate(self.owners):
                o.sem = st.enter_context(nc.semaphore("d%d" % i))
            block = st.enter_context(nc.Block())
            owners = self.owners

            def run(e, eng):
                seen = {}
                for op in self.ops[e]:
                    waits = {}
                    for d in op.deps:
                        if d[0] == "eng":
                            p = d[1]
                            if p.eng == "pe" and e == "pe" and op.dma is None:
                                continue
                            key, val = esem[p.eng], p.sigcount
                        else:
                            key, val = d[1].sem, d[2]
                        if val > waits.get(key, 0):
                            waits[key] = val
                    for key, val in waits.items():
                        if val > seen.get(key, 0):
                            eng.wait_ge(key, val)
                            seen[key] = val
                    if op.dma is not None:
                        owner, n = op.dma
                        for (o_ap, i_ap) in op.fn:
                            eng.dma_start(out=o_ap, in_=i_ap).then_inc(owner.sem, 16)
                    else:
                        ins = op.fn(eng)
                        if op.signal:
                            ins.then_inc(esem[e], 1)
                if e == final_wait_eng:
                    for o in owners:
                        if o.cnt > seen.get(o.sem, 0):
                            eng.wait_ge(o.sem, o.cnt)

            @block.tensor
            def _(eng):
                run("pe", eng)

            @block.scalar
            def _(eng):
                run("act", eng)

            @block.vector
            def _(eng):
                run("dve", eng)

            @block.gpsimd
            def _(eng):
                run("pool", eng)

            @block.sync
            def _(eng):
                run("sp", eng)


def build_program():
    nc = bass.Bass("TRN2", target_bir_lowering=False)

    def din(name, shape, dt=F32):
        return nc.dram_tensor(name, list(shape), dt, kind="ExternalInput").ap()

    def dout(name, shape):
        return nc.dram_tensor(name, list(shape), F32, kind="ExternalOutput").ap()

    def dscr(name, shape):
        return nc.dram_tensor(name, list(shape), BF16, kind="Internal").ap()

    xp = din("xp", [2048, 1024])
    xs = din("xs", [128, 1024])
    mem = din("mem", [256, 1024])
    ck = din("ck", [16, 256, 256])
    cv = din("cv", [16, 256, 256])
    spst = din("spst", [16, 15, 256])
    g_mix = din("g_mix", [1024])
    w_in = din("w_in", [1024, 4608])
    g_v = din("g_v", [512])
    b_v = din("b_v", [512])
    w_s = din("w_s", [4, 128, 128])
    b_s = din("b_s", [4, 128])
    w_pool = din("w_pool", [4, 64, 64])
    pool_scale = din("pool_scale", [256])
    g_mem = din("g_mem", [1024])
    w_kv = din("w_kv", [1024, 512])
    w_out_a = din("w_out_a", [512, 1024])
    w_out_b = din("w_out_b", [256, 1024])
    w_out_c = din("w_out_c", [256, 1024])
    w_o = din("w_o", [1024, 1024])
    g_ffn = din("g_ffn", [1024])
    w_up = din("w_up", [1024, 4096])
    w_down = din("w_down", [4096, 1024])
    g_final = din("g_final", [1024])

    yp = dout("yp", [2048, 1024])
    ys = dout("ys", [128, 1024])
    mk = dout("mk", [256, 256])
    mv = dout("mv", [256, 256])
    pool_p = dout("pool_p", [15, 256])
    pool_s = dout("pool_s", [16, 15, 256])
    cv_p = dout("cv_p", [128, 512])
    cv_s = dout("cv_s", [128, 512])

    s_all = dscr("s_all", [29, 128, 4096])

    st = contextlib.ExitStack()
    with st, nc.allow_non_contiguous_dma(reason="small strided constant loads"), \
            nc.allow_low_precision(reason="bf16 matmul operands, fp32 accumulate"):
        s = Sched(nc)

        def sb(name, shape, dt):
            return st.enter_context(nc.sbuf_tensor(name, list(shape), dt))

        xres = [sb("xres%d" % t, [128, 1024], F32) for t in range(4)]
        xres_b = [s.buf("xres%d" % t) for t in range(4)]
        hT = sb("hT", [128, 8, 512], BF16)
        hT_bb = [[s.buf("hT%d_%d" % (t, h)) for h in range(2)] for t in range(4)]
        pT = sb("pT", [128, 2, 528], F32)
        pT_b = s.buf("pT")
        pTs = sb("pTs", [128, 2, 368], F32)
        pTs_b = s.buf("pTs")
        arena = sb("arena", [128, 16384], BF16)
        regB = [s.buf("reg%d" % i) for i in range(32)]

        def aview(r0, nreg, k):
            return arena[:, r0 * 512:(r0 + nreg) * 512].rearrange("p (k c) -> p k c", k=k)

        CTX = {
            "prompt": dict(
                merged=aview(0, 8, 8), merged_b=[regB[m] for m in range(8)],
                probs=aview(8, 8, 8), probs_b=[[regB[8 + 4 * hp + i] for i in range(4)] for hp in range(2)],
                apre=aview(16, 4, 4), apre_b=[regB[16 + g] for g in range(4)],
                vh16=aview(20, 4, 4), vh16_b=[regB[20 + t] for t in range(4)],
                pooled=aview(24, 2, 2), pooled_b=[regB[24], regB[25]],
                pm=aview(26, 2, 2), pm_b=[regB[26], regB[27]],
                qT=aview(28, 2, 2), qT_b=[regB[28], regB[29]],
                oT=aview(30, 2, 2), oT_b=[regB[30], regB[31]]),
            "sample": dict(
                merged=sb("merged_s", [128, 8, 128], BF16), merged_b=[s.buf("merged_s%d" % m) for m in range(8)],
                probs=sb("probs_s", [128, 8, 128], BF16), probs_b=[s.buf("probs_s%d" % i) for i in range(2)],
                apre=sb("apre_s", [128, 4, 128], BF16), apre_b=[s.buf("apre_s%d" % g) for g in range(4)],
                vh16=sb("vh16_s", [128, 1, 512], BF16), vh16_b=[s.buf("vh16_s")],
                pooled=sb("pooled_s", [128, 2, 128], BF16), pooled_b=s.buf("pooled_s"),
                pm=sb("pm_s", [128, 2, 128], BF16), pm_b=s.buf("pm_s"),
                qT=sb("qT_s", [128, 2, 128], BF16), qT_b=s.buf("qT_s"),
                oT=sb("oT_s", [128, 2, 128], BF16), oT_b=[s.buf("oT_s%d" % i) for i in range(2)]),
        }
        merged = CTX["prompt"]["merged"]
        merged_b = CTX["prompt"]["merged_b"]
        R = sb("R", [128, 32 * 512], BF16)
        R_b = [s.buf("R%d" % i) for i in range(32)]
        junk = sb("junk", [128, 1024], BF16)
        stats = sb("stats", [128, 512], F32)
        stats_init = s.buf("stats_init")
        slots = [sb("slot%d" % i, [128, 4096], BF16) for i in range(NSLOT)]
        slot_b = [s.buf("slot%d" % i) for i in range(NSLOT)]
        f512 = [sb("f512_%d" % i, [128, 512], F32) for i in range(NF512)]
        f512_b = [s.buf("f512_%d" % i) for i in range(NF512)]
        f1024 = [sb("f1024_%d" % i, [128, 1024], F32) for i in range(NF1024)]
        f1024_b = [s.buf("f1024_%d" % i) for i in range(NF1024)]
        ident = sb("ident", [128, 128], F32)
        maskT = sb("maskT", [128, 128], F32)
        wsT_p = sb("wsT_p", [128, 4, 128], BF16)
        wsT_s = sb("wsT_s", [128, 4, 128], BF16)
        bs_p = sb("bs_p", [128, 4, 128], F32)
        bs_s = sb("bs_s", [128, 4, 128], F32)
        gvb = sb("gvb", [128, 512], F32)
        bvb = sb("bvb", [128, 512], F32)
        gfb = sb("gfb", [128, 1024], F32)
        gcols = sb("gcols", [128, 3, 8], F32)
        pscol = sb("pscol", [128, 2], F32)
        invw = sb("invw", [128, 2], F32)
        wcol = sb("wcol", [128, 2], F32)
        invcnt = sb("invcnt", [128, 2, 16], F32)
        wpbd = sb("wpbd", [128, 2, 128], BF16)
        kT_p = sb("kT_p", [128, 2, 256], BF16)
        v_p = sb("v_p", [128, 2, 256], BF16)
        ones = sb("ones", [128, 64], BF16)
        cst_b = s.buf("consts")
        cst2_b = s.buf("consts2")

        banks = [st.enter_context(nc.psum_tensor("bank%d" % i, [128, 512], F32)) for i in range(8)]
        bank_b = [s.buf("bank%d" % i) for i in range(8)]

        rr = {"bank": 0, "f512": 0, "f1024": 0, "stat": 0, "alt": 0}

        def pb():
            i = rr["bank"]
            rr["bank"] = (i + 1) % 8
            return banks[i], bank_b[i]

        def t512():
            i = rr["f512"]
            rr["f512"] = (i + 1) % NF512
            return f512[i], f512_b[i]

        def t1024():
            i = rr["f1024"]
            rr["f1024"] = (i + 1) % NF1024
            return f1024[i], f1024_b[i]

        def statcols(n):
            i = rr["stat"]
            rr["stat"] = i + n
            assert rr["stat"] <= 512
            return i

        def alt():
            rr["alt"] ^= 1
            return "act" if rr["alt"] else "dve"

        def MM(out, lhsT, rhs, start, stop, R_, W_):
            s.op("pe", lambda e: e.matmul(out, lhsT=lhsT, rhs=rhs, start=start, stop=stop), R_, W_)

        def TR(out, in_, idn, R_, W_):
            s.op("pe", lambda e: e.transpose(out=out, in_=in_, identity=idn), R_, W_)

        def ACT(out, in_, func, R_, W_, **kw):
            s.op("act", lambda e: e.activation(out=out, in_=in_, func=func, **kw), R_, W_)

        route = {"pool2dve": False}

        def rt(eng):
            return "dve" if (eng == "pool" and route["pool2dve"]) else eng

        def TT(eng, out, in0, in1, op, R_, W_):
            eng = rt(eng)
            s.op(eng, lambda e: e.tensor_tensor(out=out, in0=in0, in1=in1, op=op), R_, W_)

        def TS(eng, out, in0, s1, s2, op0, op1, R_, W_):
            eng = rt(eng)
            if s2 is None:
                s.op(eng, lambda e: e.tensor_scalar(out=out, in0=in0, scalar1=s1, scalar2=None, op0=op0), R_, W_)
            else:
                s.op(eng, lambda e: e.tensor_scalar(out=out, in0=in0, scalar1=s1, scalar2=s2, op0=op0, op1=op1), R_, W_)

        def STT(eng, out, in0, scalar, in1, op0, op1, R_, W_):
            eng = rt(eng)
            s.op(eng, lambda e: e.scalar_tensor_tensor(out=out, in0=in0, scalar=scalar, in1=in1, op0=op0, op1=op1), R_, W_)

        def CP(eng, out, in_, R_, W_):
            eng = rt(eng)
            if eng == "act":
                s.op("act", lambda e: e.activation(out=out, in_=in_, func=AF.Identity), R_, W_)
            else:
                s.op(eng, lambda e: e.tensor_copy(out=out, in_=in_), R_, W_)

        def MEMSET(eng, ap, val, W_):
            eng = rt(eng)
            s.op(eng, lambda e: e.memset(ap, val), (), W_)

        def kview(ap, k):
            return ap.rearrange("(k p) c -> p k c", p=128)

        def chunk_pairs(name, dst):
            d8 = dst.rearrange("p (k c) -> p k c", k=8)
            if name == "wkv":
                return [(d8, kview(w_kv, 8))]
            if name.startswith("win"):
                c = int(name[3:])
                return [(d8, kview(w_in[:, c * 512:(c + 1) * 512], 8))]
            if name.startswith("mrg"):
                m = int(name[3:])
                prs = []
                gv_ = dst[:, 0:3072].rearrange("p (k b c) -> p k b c", k=8, b=3)
                for br in range(3):
                    c0 = 1536 + br * 1024 + m * 128
                    prs.append((gv_[:, :, br, :], kview(w_in[:, c0:c0 + 128], 8)))
                prs.append((dst[:, 3072:3584].rearrange("p (k c) -> p k c", k=4), kview(w_out_a[:, m * 128:(m + 1) * 128], 4)))
                prs.append((dst[:, 3584:3840].rearrange("p (k c) -> p k c", k=2), kview(w_out_b[:, m * 128:(m + 1) * 128], 2)))
                prs.append((dst[:, 3840:4096].rearrange("p (k c) -> p k c", k=2), kview(w_out_c[:, m * 128:(m + 1) * 128], 2)))
                return prs
            if name.startswith("wo"):
                n = int(name[2:])
                return [(d8, kview(w_o[:, n * 512:(n + 1) * 512], 8))]
            if name.startswith("wup"):
                c = int(name[3:])
                return [(d8, kview(w_up[:, c * 512:(c + 1) * 512], 8))]
            kc, n = name[3:].split("_")
            kc, n = int(kc), int(n)
            return [(d8, kview(w_down[kc * 1024:(kc + 1) * 1024, n * 512:(n + 1) * 512], 8))]

        block_chunks = (["win1", "win0", "win2"] + ["mrg%d" % m for m in range(8)] + ["wo0", "wo1"] +
                        ["wup%d" % c for c in range(8)] + ["wdn%d_%d" % (kc, n) for kc in range(4) for n in range(2)])
        stream = block_chunks * 5
        stream.insert(3, "wkv")
        sinfo = []
        _n = 0
        for nm_ in stream:
            if nm_ == "wkv":
                sinfo.append((0, -1))
            else:
                sinfo.append((_n // len(block_chunks), _n % len(block_chunks)))
                _n += 1
        chunk_idx = {nm: i for i, nm in enumerate(block_chunks)}
        scr = {nm: s.buf("scr_" + nm) for nm in block_chunks}
        ws = {"next_load": 0, "pos": 0, "wb": None}
        slot_sw = [s.buf("slot_sw%d" % i) for i in range(NSLOT)]

        def wget(expect, hold=0):
            i = ws["pos"]
            assert stream[i] == expect, (stream[i], expect)
            ws["pos"] = i + 1
            while ws["next_load"] < len(stream) and ws["next_load"] <= i + NSLOT - 1 - hold:
                j = ws["next_load"]
                nm = stream[j]
                sl = j % NSLOT
                blk_i, ci = sinfo[j]
                do_cast = (ci < 0) or (blk_i == 0) or (blk_i == 1 and ci % 2 == 1)
                do_wb = (ci >= 0) and ((blk_i == 0 and ci % 2 == 0) or (blk_i == 1 and ci % 2 == 1))
                if do_cast:
                    s.dma("pool", chunk_pairs(nm, slots[sl][:, :]), writes=[slot_b[sl]], owner=slot_sw[sl])
                    if ws["wb"] is not None:
                        pn, psl = ws["wb"]
                        s.dma("pool", [(s_all[chunk_idx[pn]], slots[psl][:, :])], reads=[slot_b[psl]], writes=[scr[pn]], owner=scr[pn])
                    ws["wb"] = (nm, sl) if do_wb else None
                else:
                    if ws["wb"] is not None:
                        pn, psl = ws["wb"]
                        s.dma("pool", [(s_all[chunk_idx[pn]], slots[psl][:, :])], reads=[slot_b[psl]], writes=[scr[pn]], owner=scr[pn])
                        ws["wb"] = None
                    s.dma("sp", [(slots[sl][:, :], s_all[chunk_idx[nm]])], reads=[scr[nm]], writes=[slot_b[sl]], owner=slot_b[sl])
                ws["next_load"] = j + 1
            sl = i % NSLOT
            return slots[sl], slot_b[sl]

        def pbc(ap1d, n):
            return ap1d.partition_broadcast(128)

        cpairs = [
            (bs_p[:], b_s.partition_broadcast(128)),
            (gvb[:], g_v.partition_broadcast(128)),
            (bvb[:], b_v.partition_broadcast(128)),
            (gfb[:], g_final.partition_broadcast(128)),
        ]
        s.dma("sp", cpairs, writes=[cst_b], owner=cst_b)

        ident_b = s.buf("ident")
        mask_b = s.buf("mask")
        MEMSET("pool", ident[:], 1.0, [ident_b])
        s.op("pool", lambda e: e.affine_select(out=ident[:], in_=ident[:], compare_op=ALU.is_equal, fill=0.0, base=0,
                                               pattern=[[-1, 128]], channel_multiplier=1), [ident_b], [ident_b])
        MEMSET("pool", maskT[:], 1.0, [mask_b])
        s.op("pool", lambda e: e.affine_select(out=maskT[:], in_=maskT[:], compare_op=ALU.is_ge, fill=0.0, base=0,
                                               pattern=[[1, 128]], channel_multiplier=-1), [mask_b], [mask_b])

        def mk_small(e):
            e.memset(stats[:], 0.0)
            e.memset(ones[:], 1.0)
            e.memset(invw[0:64, 0:1], 0.5)
            e.memset(invw[64:128, 0:1], 0.25)
            e.memset(invw[0:64, 1:2], 0.125)
            e.memset(invw[64:128, 1:2], 0.0625)
            e.memset(wcol[0:64, 0:1], 2.0)
            e.memset(wcol[64:128, 0:1], 4.0)
            e.memset(wcol[0:64, 1:2], 8.0)
            e.memset(wcol[64:128, 1:2], 16.0)
            ins = None
            for t in range(16):
                ins = e.memset(invcnt[:, :, t:t + 1], float(t + 1))
            return ins

        s.op("pool", mk_small, (), [cst2_b, stats_init])
        for ft in range(2):
            TS("pool", invcnt[:, ft, :], invcnt[:, ft, :], wcol[:, ft:ft + 1], None, ALU.min, None, [cst2_b], [cst2_b])
        s.op("dve", lambda e: e.reciprocal(out=invcnt[:], in_=invcnt[:]), [cst2_b], [cst2_b])

        gst, gst_b = t512()
        s.dma("sp", [(gst[0:8, 0:128], g_mix.rearrange("(k p) -> k p", p=128)),
                     (gst[0:8, 128:256], g_ffn.rearrange("(k p) -> k p", p=128)),
                     (gst[0:8, 256:384], g_mem.rearrange("(k p) -> k p", p=128)),
                     (gst[0:2, 384:512], pool_scale.rearrange("(k p) -> k p", p=128))], writes=[gst_b], owner=gst_b)
        gcol_b = s.buf("gcol")
        bk, bk_b = pb()
        for i in range(3):
            TR(bk[:, i * 8:(i + 1) * 8], gst[0:8, i * 128:(i + 1) * 128], ident[0:8, 0:8], [gst_b, ident_b], [bk_b])
        TR(bk[:, 24:26], gst[0:2, 384:512], ident[0:2, 0:2], [gst_b, ident_b], [bk_b])
        CP("dve", gcols[:].rearrange("p a b -> p (a b)"), bk[:, 0:24], [bk_b], [gcol_b])
        CP("dve", pscol[:], bk[:, 24:26], [bk_b], [gcol_b])

        wpst, wpst_b = t512()
        MEMSET("pool", wpst[:, 0:256], 0.0, [wpst_b])
        wpv = wpst[:, 0:256].rearrange("p (t c) -> p t c", t=2)
        s.dma("sp", [(wpv[(g % 2) * 64:(g % 2) * 64 + 64, g // 2, (g % 2) * 64:(g % 2) * 64 + 64], w_pool[g]) for g in range(4)],
              writes=[wpst_b], owner=wpst_b)
        wpbd_b = s.buf("wpbd")
        CP("pool", wpbd[:], wpv, [wpst_b], [wpbd_b])

        wsn, wsn_b = t512()
        s.dma("sp", [(wsn[:].rearrange("p (g j) -> p g j", g=4), w_s.rearrange("g i j -> i g j"))], writes=[wsn_b], owner=wsn_b)
        bk, bk_b = pb()
        for g in range(4):
            TR(bk[:, g * 128:(g + 1) * 128], wsn[:, g * 128:(g + 1) * 128], ident[:], [wsn_b, ident_b], [bk_b])
        wsT_b = s.buf("wsT")
        wsTs_b = s.buf("wsTs")
        for g in range(4):
            TT("dve", wsT_p[:, g, :], bk[:, g * 128:(g + 1) * 128], maskT[:], ALU.mult, [bk_b, mask_b], [wsT_b])

        csts_b = s.buf("consts_s")
        wss = bs_s_stage = sb("wss", [128, 512], F32)
        wss_b = s.buf("wss")
        wssv = wss[:].rearrange("p (g i) -> p g i", g=4)

        def sample_consts():
            s.dma("sp", [(wssv[0:8, g, 0:8], w_s[g, 0:8, 0:8].rearrange("i j -> j i")) for g in range(4)],
                  writes=[wss_b], owner=wss_b)

        def sample_consts_ops():
            DBL = (8, 16, 32, 64)
            for w in DBL:
                CP("pool", wssv[0:8, :, w:2 * w], wssv[0:8, :, 0:w], [wss_b], [wss_b])
            Lt, Lt_b = t512()
            CP("pool", Lt[0:8, 0:8], ident[0:8, 0:8], [ident_b], [Lt_b])
            for w in DBL:
                CP("pool", Lt[0:8, w:2 * w], Lt[0:8, 0:w], [Lt_b], [Lt_b])
            Et, Et_b = t512()
            MEMSET("pool", Et[0:16, 0:128], 1.0, [Et_b])
            s.op("pool", lambda e: e.affine_select(out=Et[0:16, 0:128], in_=Et[0:16, 0:128], compare_op=ALU.is_ge, fill=0.0,
                                                   base=0, pattern=[[1, 128]], channel_multiplier=-8), [Et_b], [Et_b])
            s.op("pool", lambda e: e.affine_select(out=Et[0:16, 0:128], in_=Et[0:16, 0:128], compare_op=ALU.is_ge, fill=0.0,
                                                   base=7, pattern=[[-1, 128]], channel_multiplier=8), [Et_b], [Et_b])
            bk, bk_b = pb()
            MM(bk[:, 0:128], Et[0:16, 0:128], Et[0:16, 0:128], True, True, [Et_b], [bk_b])
            M2, M2_b = t512()
            TT("dve", M2[:, 0:128], bk[:, 0:128], maskT[:], ALU.mult, [bk_b, mask_b], [M2_b])
            bk2, bk2_b = pb()
            for g in range(4):
                MM(bk2[:, g * 128:(g + 1) * 128], Lt[0:8, 0:128], wssv[0:8, g, :], True, True, [Lt_b, wss_b], [bk2_b])
            for g in range(4):
                TT("dve", wsT_s[:, g, :], bk2[:, g * 128:(g + 1) * 128], M2[:, 0:128], ALU.mult, [bk2_b, M2_b], [wsTs_b])
            CP("pool", bs_s[:, :, 0:8], bs_p[:, :, 0:8], [cst_b], [csts_b])
            for w in DBL:
                CP("pool", bs_s[:, :, w:2 * w], bs_s[:, :, 0:w], [csts_b], [csts_b])

        def rms_rstd(x_ap, x_bufs, width, power=-0.5):
            c = statcols(4)
            sbuf_ = s.buf("st%d" % c)
            ACT(junk[:, 0:width], x_ap, AF.Square, list(x_bufs) + [stats_init], [sbuf_], accum_out=stats[:, c:c + 1])
            TS("dve", stats[:, c + 1:c + 2], stats[:, c:c + 1], 1.0 / width, EPS, ALU.mult, ALU.add, [sbuf_], [sbuf_])
            ACT(stats[:, c + 2:c + 3], stats[:, c + 1:c + 2], AF.Ln, [sbuf_], [sbuf_])
            ACT(stats[:, c + 3:c + 4], stats[:, c + 2:c + 3], AF.Exp, [sbuf_], [sbuf_], scale=power)
            return stats[:, c + 3:c + 4], sbuf_

        def norm_a(x_ap, x_buf, inplace):
            rstd, rb = rms_rstd(x_ap, [x_buf], 1024)
            if inplace is not None:
                xn, xn_b = inplace
            else:
                xn, xn_b = t1024()
            TS("dve", xn[:], x_ap, rstd, None, ALU.mult, None, [x_buf, rb], [xn_b])
            return xn, xn_b

        def norm_b(xn, xn_b, gidx, dstT, dst_bufs, col0):
            for half in range(2):
                bk, bk_b = pb()
                for kk in range(4):
                    k = half * 4 + kk
                    TR(bk[:, kk * 128:(kk + 1) * 128], xn[:, k * 128:(k + 1) * 128], ident[:], [xn_b, ident_b], [bk_b])
                for kk in range(4):
                    k = half * 4 + kk
                    gc = gcols[:, gidx, k:k + 1]
                    o = dstT[:, k, col0:col0 + 128]
                    i_ = bk[:, kk * 128:(kk + 1) * 128]
                    if half == 0:
                        ACT(o, i_, AF.Identity, [bk_b, gcol_b], dst_bufs[half], scale=gc)
                    else:
                        TS("dve", o, i_, gc, None, ALU.mult, None, [bk_b, gcol_b], dst_bufs[half])

        def norm_T(x_ap, x_buf, gidx, dstT, dst_bufs, col0):
            xn, xn_b = norm_a(x_ap, x_buf, None)
            norm_b(xn, xn_b, gidx, dstT, dst_bufs, col0)

        hmT = merged
        hm_l = list(merged_b)
        kvp_b = s.buf("kvp")
        mem_xn = []

        def mem_prep_early():
            for t in range(2):
                s.dma("sp", [(xres[2 + t][:], mem[t * 128:(t + 1) * 128, :])], writes=[xres_b[2 + t]], owner=xres_b[2 + t])
                mem_xn.append(norm_a(xres[2 + t][:], xres_b[2 + t], (xres[2 + t], xres_b[2 + t])))

        def mem_prep_pe():
            for t in range(2):
                norm_b(mem_xn[t][0], mem_xn[t][1], 2, hmT, [hm_l, hm_l], t * 128)
            wk, wk_b = wget("wkv")
            wkv3 = wk[:, :].rearrange("p (k c) -> p k c", k=8)
            for t in range(2):
                bk, bk_b = pb()
                for k in range(8):
                    MM(bk[:, :], hmT[:, k, t * 128:(t + 1) * 128], wkv3[:, k, :], k == 0, k == 7, hm_l + [wk_b], [bk_b])
                kvf, kvf_b = t512()
                CP("dve", kvf[:], bk[:, :], [bk_b], [kvf_b])
                CP("pool", v_p[:, t, :], kvf[:, 256:512], [kvf_b], [kvp_b])
                s.dma("sp", [(mk[t * 128:(t + 1) * 128, :], kvf[:, 0:256]), (mv[t * 128:(t + 1) * 128, :], kvf[:, 256:512])],
                      reads=[kvf_b], owner=kvf_b)
            for hp in range(2):
                bk, bk_b = pb()
                for k in range(8):
                    MM(bk[:, 0:256], wkv3[:, k, hp * 128:(hp + 1) * 128], hmT[:, k, 0:256], k == 0, k == 7, hm_l + [wk_b], [bk_b])
                CP("dve", kT_p[:, hp, :], bk[:, 0:256], [bk_b], [kvp_b])

        def sample_kv_prep(b):
            t1, t1_b = t512()
            s.dma("sp", [(t1[:].rearrange("p (c f) -> p c f", c=2), ck[b].rearrange("(c m) f -> m c f", m=128))],
                  writes=[t1_b], owner=t1_b)
            bk, bk_b = pb()
            for c in range(2):
                for hp in range(2):
                    TR(bk[:, hp * 256 + c * 128: hp * 256 + (c + 1) * 128], t1[:, c * 256 + hp * 128: c * 256 + (hp + 1) * 128],
                       ident[:], [t1_b, ident_b], [bk_b])
            CP("dve", arena[:, b * 512:(b + 1) * 512], bk[:, :], [bk_b], [regB[b]])
            t2, t2_b = t512()
            s.dma("sp", [(t2[:].rearrange("p (c f) -> p c f", c=2), cv[b].rearrange("(c m) f -> m c f", m=128))],
                  writes=[t2_b], owner=t2_b)
            CP("pool", arena[:, (16 + b) * 512:(17 + b) * 512], t2[:], [t2_b], [regB[16 + b]])

        def pTs3(ft):
            return pTs[:, ft, :].rearrange("p (b c) -> p b c", c=23)

        cpy_b = s.buf("poolcopy")

        def sample_hist_prep():
            for rt_ in range(2):
                tS, tS_b = t512()
                s.dma("sp", [(tS[0:120, 0:256], spst[rt_ * 8:(rt_ + 1) * 8].rearrange("b r c -> (b r) c"))], writes=[tS_b], owner=tS_b)
                bk, bk_b = pb()
                for ft in range(2):
                    TR(bk[:, ft * 128: ft * 128 + 120], tS[0:120, ft * 128:(ft + 1) * 128], ident[0:120, 0:120], [tS_b, ident_b], [bk_b])
                for ft in range(2):
                    CP("dve", pTs3(ft)[:, rt_ * 8:(rt_ + 1) * 8, 0:15],
                       bk[:, ft * 128: ft * 128 + 120].rearrange("p (b r) -> p b r", r=15), [bk_b], [pTs_b])
            s.dma("sp", [(pool_s[:, 0:7, :], spst[:, 8:15, :])], owner=cpy_b)

        def blk_info(kind, j):
            sample = kind == "sample"
            NT = 1 if sample else 4
            return sample, NT

        def P0a(kind, j):
            sample, NT = blk_info(kind, j)
            res = []
            for t in range(NT):
                src = xs if sample else xp[j * 512 + t * 128: j * 512 + (t + 1) * 128, :]
                T_, T_b = t1024()
                s.dma("sp", [(T_[:], src)], writes=[T_b], owner=T_b)
                res.append(norm_a(T_[:], T_b, (T_, T_b)))
            return res

        def P0b(kind, j, xns):
            sample, NT = blk_info(kind, j)
            for t in range(NT):
                norm_b(xns[t][0], xns[t][1], 0, hT, [[hT_bb[t][0]], [hT_bb[t][1]]], t * 128)

        def run_main(kind, j, nxt):
            sample = kind == "sample"
            NT = 1 if sample else 4
            cols = 128 * NT
            hT_all = [hT_bb[t][h] for t in range(NT) for h in range(2)]
            last = (not sample) and j == 3
            Rt, Rt_b = R, R_b
            rw = 128 if sample else 512
            C = CTX[kind]
            merged, merged_b, probs, probs_b = C["merged"], C["merged_b"], C["probs"], C["probs_b"]
            apre, apre_b, vh16, vh16_b = C["apre"], C["apre_b"], C["vh16"], C["vh16_b"]
            pooled, pooled_b, pm, pm_b = C["pooled"], C["pooled_b"], C["pm"], C["pm_b"]
            qT, qT_b, oT, oT_b = C["qT"], C["qT_b"], C["oT"], C["oT_b"]
            pTb_ = pTs_b if sample else pT_b
            route["pool2dve"] = (not sample) and j <= 1

            first_blk = (not sample) and j == 0

            def load_xres(t):
                src = xs if sample else xp[j * 512 + t * 128: j * 512 + (t + 1) * 128, :]
                s.dma("sp", [(xres[t][:], src)], writes=[xres_b[t]], owner=xres_b[t])

            for t in range(NT):
                if not (first_blk and t >= 2):
                    load_xres(t)

            w1, w1_b = wget("win1", hold=(3 if first_blk else 0))
            w13 = w1[:, :].rearrange("p (k c) -> p k c", k=8)
            gvs = []
            c0 = statcols(3 * 4)
            lnb = s.buf("lnstats%d" % c0)
            for t in range(NT):
                bk, bk_b = pb()
                for k in range(8):
                    MM(bk[:, :], hT[:, k, t * 128:(t + 1) * 128], w13[:, k, :], k == 0, k == 7, hT_bb[t] + [w1_b], [bk_b])
                gv, gv_b = t512()
                ACT(gv[:], bk[:, :], AF.Gelu_apprx_tanh, [bk_b], [gv_b])
                gvs.append((gv, gv_b))
            mvt = stats[:, c0:c0 + 8].rearrange("p (t c) -> p t c", c=2)
            c6 = statcols(6 * 4)
            for t in range(NT):
                gv, gv_b = gvs[t]
                st6b = s.buf("st6_%d_%d" % (c6, t))
                s.op("dve", lambda e, o=stats[:, c6 + 6 * t:c6 + 6 * t + 6], i=gv[:]: e.bn_stats(out=o, in_=i), [gv_b, stats_init], [st6b])
                s.op("dve", lambda e, o=mvt[:, t, :], i=stats[:, c6 + 6 * t:c6 + 6 * t + 6]: e.bn_aggr(out=o, in_=i), [st6b, stats_init], [lnb])
            TS("dve", stats[:, c0 + 8:c0 + 8 + NT], mvt[:, 0:NT, 1], EPS, None, ALU.add, None, [lnb], [lnb])
            ACT(stats[:, c0 + 8:c0 + 8 + NT], stats[:, c0 + 8:c0 + 8 + NT], AF.Ln, [lnb], [lnb])
            ACT(stats[:, c0 + 8:c0 + 8 + NT], stats[:, c0 + 8:c0 + 8 + NT], AF.Exp, [lnb], [lnb], scale=-0.5)
            for t in range(NT):
                gv, gv_b = gvs[t]
                eng2 = "dve" if (t % 2 == 0) else "pool"
                TS("dve", gv[:], gv[:], mvt[:, t, 0:1], stats[:, c0 + 8 + t:c0 + 9 + t], ALU.subtract, ALU.mult, [gv_b, lnb], [gv_b])
                TT(eng2, gv[:], gv[:], gvb[:], ALU.mult, [gv_b, cst_b], [gv_b])
                if sample or (last and t == 3):
                    TT(eng2, gv[:], gv[:], bvb[:], ALU.add, [gv_b, cst_b], [gv_b])
                    s.dma("sp", [((cv_s if sample else cv_p)[:, :], gv[:])], reads=[gv_b], owner=gv_b)
                    CP(eng2, vh16[:, t, :], gv[:], [gv_b], [vh16_b[t]])
                else:
                    TT(eng2, vh16[:, t, :], gv[:], bvb[:], ALU.add, [gv_b, cst_b], [vh16_b[t]])

            w0, w0_b = wget("win0")
            w03 = w0[:, :].rearrange("p (k c) -> p k c", k=8)
            ugs = []
            for g in range(4):
                bk, bk_b = pb()
                for k in range(8):
                    MM(bk[:, 0:cols], w03[:, k, g * 128:(g + 1) * 128], hT[:, k, 0:cols], k == 0, k == 7, hT_all + [w0_b], [bk_b])
                ug, ug_b = t512()
                ACT(ug[:, 0:cols], bk[:, 0:cols], AF.Gelu_apprx_tanh, [bk_b], [ug_b])
                ugs.append((ug, ug_b))

            w2, w2_b = wget("win2")
            w23 = w2[:, :].rearrange("p (k c) -> p k c", k=8)
            if not sample:
                if j == 0:
                    MEMSET("pool", pT[:, :, 0:16], 0.0, [pTb_])
                else:
                    CP("pool", pT[:, :, 1:16], pT[:, :, 513:528], [pTb_], [pTb_])
            for ft in range(2):
                bk, bk_b = pb()
                for k in range(8):
                    MM(bk[:, 0:cols], w23[:, k, ft * 128:(ft + 1) * 128], hT[:, k, 0:cols], k == 0, k == 7, hT_all + [w2_b], [bk_b])
                if sample:
                    CP("dve", pTs3(ft)[:, :, 15:23], bk[:, 0:128].rearrange("p (b l) -> p b l", l=8), [bk_b], [pTb_])
                else:
                    CP("dve", pT[:, ft, 16:528], bk[:, :], [bk_b], [pTb_])
            for ft in range(2):
                bk, bk_b = pb()
                for k in range(8):
                    MM(bk[:, 0:cols], w23[:, k, 256 + ft * 128: 256 + (ft + 1) * 128], hT[:, k, 0:cols], k == 0, k == 7,
                       hT_all + [w2_b], [bk_b])
                CP("act", qT[:, ft, 0:cols], bk[:, 0:cols], [bk_b], [qT_b])
            if sample or last:
                tt = 0 if sample else 3
                bk, bk_b = pb()
                for k in range(8):
                    MM(bk[:, 0:256], hT[:, k, tt * 128:(tt + 1) * 128], w23[:, k, 0:256], k == 0, k == 7, hT_bb[tt] + [w2_b], [bk_b])
                ptk, ptk_b = t512()
                CP("act", ptk[:, 0:256], bk[:, 0:256], [bk_b], [ptk_b])
                if sample:
                    s.dma("sp", [(pool_s[b, 7:15, :], ptk[b * 8:(b + 1) * 8, 0:256]) for b in range(16)], reads=[ptk_b], owner=ptk_b)
                else:
                    s.dma("sp", [(pool_p[:, :], ptk[113:128, 0:256])], reads=[ptk_b], owner=ptk_b)

            if first_blk:
                mem_prep_pe()
                load_xres(2)
                load_xres(3)

            if sample:
                def P3(ft):
                    return pTs3(ft)

                def V3(tile):
                    return tile[:, 0:368].rearrange("p (b c) -> p b c", c=23)

                def O3(ft):
                    return pooled[:, ft, 0:128].rearrange("p (b l) -> p b l", l=8)

                A, A_b = t1024()
                Bt, Bt_b = t1024()
                TT("pool", V3(A)[:, :, 1:23], P3(0)[:, :, 1:23], P3(0)[:, :, 0:22], ALU.add, [pTb_], [A_b])
                TT("pool", V3(Bt)[64:128, :, 3:23], V3(A)[64:128, :, 3:23], V3(A)[64:128, :, 1:21], ALU.add, [A_b], [Bt_b])
                STT("dve", O3(0)[0:64], V3(A)[0:64, :, 15:23], invw[0:64, 0:1], P3(0)[0:64, :, 15:23], ALU.mult, ALU.subtract,
                    [A_b, pTb_, cst2_b], [pooled_b])
                STT("dve", O3(0)[64:128], V3(Bt)[64:128, :, 15:23], invw[64:128, 0:1], P3(0)[64:128, :, 15:23], ALU.mult, ALU.subtract,
                    [Bt_b, pTb_, cst2_b], [pooled_b])
                A2, A2_b = t1024()
                B2, B2_b = t1024()
                TT("pool", V3(A2)[:, :, 1:23], P3(1)[:, :, 1:23], P3(1)[:, :, 0:22], ALU.add, [pTb_], [A2_b])
                TT("pool", V3(B2)[:, :, 3:23], V3(A2)[:, :, 3:23], V3(A2)[:, :, 1:21], ALU.add, [A2_b], [B2_b])
                C2, C2_b = t1024()
                TT("pool", V3(C2)[:, :, 7:23], V3(B2)[:, :, 7:23], V3(B2)[:, :, 3:19], ALU.add, [B2_b], [C2_b])
                D2, D2_b = t1024()
                TT("pool", V3(D2)[64:128, :, 15:23], V3(C2)[64:128, :, 15:23], V3(C2)[64:128, :, 7:15], ALU.add, [C2_b], [D2_b])
                STT("dve", O3(1)[0:64], V3(C2)[0:64, :, 15:23], invw[0:64, 1:2], P3(1)[0:64, :, 15:23], ALU.mult, ALU.subtract,
                    [C2_b, pTb_, cst2_b], [pooled_b])
                STT("dve", O3(1)[64:128], V3(D2)[64:128, :, 15:23], invw[64:128, 1:2], P3(1)[64:128, :, 15:23], ALU.mult, ALU.subtract,
                    [D2_b, pTb_, cst2_b], [pooled_b])
            else:
                fxs = []

                def fin(ft, lo, hi, S_, S_b):
                    STT("dve", pooled[lo:hi, ft, :], S_[lo:hi, 16:528], invw[lo:hi, ft:ft + 1], pT[lo:hi, ft, 16:528],
                        ALU.mult, ALU.subtract, [S_b, pTb_, cst2_b], [pooled_b])
                    if j == 0:
                        if not fxs:
                            fxs.append(t512())
                        fx, fx_b = fxs[0]
                        c_ = ft * 16
                        TT("dve", fx[lo:hi, c_:c_ + 15], S_[lo:hi, 16:31], invcnt[lo:hi, ft, 0:15], ALU.mult, [S_b, cst2_b], [fx_b])
                        TT("dve", pooled[lo:hi, ft, 0:15], fx[lo:hi, c_:c_ + 15], pT[lo:hi, ft, 16:31], ALU.subtract, [fx_b, pTb_], [pooled_b])

                A, A_b = t1024()
                Bt, Bt_b = t1024()
                TT("pool", A[:, 2:528], pT[:, 0, 2:528], pT[:, 0, 1:527], ALU.add, [pTb_], [A_b])
                TT("pool", Bt[64:128, 4:528], A[64:128, 4:528], A[64:128, 2:526], ALU.add, [A_b], [Bt_b])
                fin(0, 0, 64, A, A_b)
                fin(0, 64, 128, Bt, Bt_b)
                A2, A2_b = t1024()
                B2, B2_b = t1024()
                TT("dve", A2[:, 2:528], pT[:, 1, 2:528], pT[:, 1, 1:527], ALU.add, [pTb_], [A2_b])
                TT("dve", B2[:, 4:528], A2[:, 4:528], A2[:, 2:526], ALU.add, [A2_b], [B2_b])
                C2, C2_b = t1024()
                TT("dve", C2[:, 8:528], B2[:, 8:528], B2[:, 4:524], ALU.add, [B2_b], [C2_b])
                D2, D2_b = t1024()
                TT("dve", D2[64:128, 16:528], C2[64:128, 16:528], C2[64:128, 8:520], ALU.add, [C2_b], [D2_b])
                fin(1, 0, 64, C2, C2_b)
                fin(1, 64, 128, D2, D2_b)
            for hp in range(2):
                pr_b = probs_b[hp]
                if sample:
                    for hh in range(2):
                        bk, bk_b = pb()
                        for b in range(16):
                            kTb = arena[:, b * 512:(b + 1) * 512].rearrange("p (h m) -> p h m", h=2)
                            for mc in range(2):
                                cc = mc * 128 + b * 8
                                MM(bk[:, cc:cc + 8], kTb[hh * 64:(hh + 1) * 64, hp, mc * 128:(mc + 1) * 128],
                                   qT[hh * 64:(hh + 1) * 64, hp, b * 8:(b + 1) * 8], True, True, [regB[b], qT_b], [bk_b])
                        ACT(probs[:, hp * 4 + hh * 2:hp * 4 + hh * 2 + 2, 0:128], bk[:, 0:256].rearrange("p (i c) -> p i c", i=2), AF.Exp,
                            [bk_b], [pr_b], scale=0.125)
                else:
                    for hh in range(2):
                        for mc in range(2):
                            bk, bk_b = pb()
                            MM(bk[:, 0:cols], kT_p[hh * 64:(hh + 1) * 64, hp, mc * 128:(mc + 1) * 128],
                               qT[hh * 64:(hh + 1) * 64, hp, 0:cols], True, True, [kvp_b, qT_b], [bk_b])
                            ACT(probs[:, hp * 4 + hh * 2 + mc, 0:cols], bk[:, 0:cols], AF.Exp, [bk_b], [pr_b], scale=0.125)
            def pool_mix():
                for ft in range(2):
                    bk, bk_b = pb()
                    MM(bk[:, 0:cols], wpbd[:, ft, :], pooled[:, ft, 0:cols], True, True, [pooled_b, wpbd_b], [bk_b])
                    ACT(pm[:, ft, 0:cols], bk[:, 0:cols], AF.Identity, [bk_b, gcol_b], [pm_b], scale=pscol[:, ft:ft + 1])

            for hp in range(2):
                pr_b = probs_b[hp]
                bo, bo_b = pb()
                bd, bd_b = pb()
                if sample:
                    for b in range(16):
                        vb = arena[:, (16 + b) * 512:(17 + b) * 512].rearrange("p (c f) -> p c f", c=2)
                        for hh in range(2):
                            h = 2 * hp + hh
                            for mc in range(2):
                                MM(bo[hh * 64:(hh + 1) * 64, b * 8:(b + 1) * 8], vb[:, mc, h * 64:(h + 1) * 64],
                                   probs[:, hp * 4 + hh * 2 + mc, b * 8:(b + 1) * 8], mc == 0, mc == 1, [regB[16 + b], pr_b], [bo_b])
                    for hh in range(2):
                        for mc in range(2):
                            MM(bd[hh * 64:(hh + 1) * 64, 0:128], ones[:, 0:64], probs[:, hp * 4 + hh * 2 + mc, 0:128],
                               mc == 0, mc == 1, [pr_b, cst2_b], [bd_b])
                else:
                    for hh in range(2):
                        h = 2 * hp + hh
                        for mc in range(2):
                            MM(bo[hh * 64:(hh + 1) * 64, 0:cols], v_p[:, mc, h * 64:(h + 1) * 64],
                               probs[:, hp * 4 + hh * 2 + mc, 0:cols], mc == 0, mc == 1, [kvp_b, pr_b], [bo_b])
                    for hh in range(2):
                        for mc in range(2):
                            MM(bd[hh * 64:(hh + 1) * 64, 0:cols], ones[:, 0:64], probs[:, hp * 4 + hh * 2 + mc, 0:cols],
                               mc == 0, mc == 1, [pr_b, cst2_b], [bd_b])
                rd, rd_b = t512()
                ACT(rd[:, 0:cols], bd[:, 0:cols], AF.Ln, [bd_b], [rd_b])
                ACT(rd[:, 0:cols], rd[:, 0:cols], AF.Exp, [rd_b], [rd_b], scale=-1.0)
                TT("dve", oT[:, hp, 0:cols], bo[:, 0:cols], rd[:, 0:cols], ALU.mult, [bo_b, rd_b], [oT_b[hp]])

            wsT = wsT_s if sample else wsT_p
            bsx = bs_s if sample else bs_p
            wsTb_ = wsTs_b if sample else wsT_b
            bsb_ = csts_b if sample else cst_b
            for g in range(4):
                ug, ug_b = ugs[g]
                bk, bk_b = pb()
                for t in range(NT):
                    MM(bk[:, t * 128:(t + 1) * 128], vh16[:, t, g * 128:(g + 1) * 128], wsT[:, g, :], True, True,
                       [vh16_b[t], wsTb_], [bk_b])
                tm, tm_b = t512()
                for t in range(NT):
                    TT("dve", tm[:, t * 128:(t + 1) * 128], bk[:, t * 128:(t + 1) * 128], bsx[:, g, :], ALU.add, [bk_b, bsb_], [tm_b])
                TT("pool", apre[:, g, 0:cols], tm[:, 0:cols], ug[:, 0:cols], ALU.mult, [tm_b, ug_b], [apre_b[g]])

            for m in range(8):
                wm, wm_b = wget("mrg%d" % m)
                gb = []
                for br in range(3):
                    bk, bk_b = pb()
                    for k in range(8):
                        MM(bk[:, 0:cols], wm[:, (k * 3 + br) * 128:(k * 3 + br + 1) * 128], hT[:, k, 0:cols], k == 0, k == 7,
                           hT_all + [wm_b], [bk_b])
                    gb.append((bk, bk_b))
                if m == 0:
                    pool_mix()
                ob = []
                bk, bk_b = pb()
                for k in range(4):
                    MM(bk[:, 0:cols], wm[:, 3072 + k * 128: 3072 + (k + 1) * 128], apre[:, k, 0:cols], k == 0, k == 3,
                       [apre_b[k], wm_b], [bk_b])
                ob.append((bk, bk_b))
                bk, bk_b = pb()
                for k in range(2):
                    MM(bk[:, 0:cols], wm[:, 3584 + k * 128: 3584 + (k + 1) * 128], pm[:, k, 0:cols], k == 0, k == 1, [pm_b, wm_b], [bk_b])
                ob.append((bk, bk_b))
                bk, bk_b = pb()
                for k in range(2):
                    MM(bk[:, 0:cols], wm[:, 3840 + k * 128: 3840 + (k + 1) * 128], oT[:, k, 0:cols], k == 0, k == 1, [oT_b[k], wm_b], [bk_b])
                ob.append((bk, bk_b))
                ts_ = []
                for br in range(3):
                    sg, sg_b = t512()
                    ACT(sg[:, 0:cols], gb[br][0][:, 0:cols], AF.Sigmoid, [gb[br][1]], [sg_b])
                    TT("dve", sg[:, 0:cols], ob[br][0][:, 0:cols], sg[:, 0:cols], ALU.mult, [ob[br][1], sg_b], [sg_b])
                    ts_.append((sg, sg_b))
                TT("pool", ts_[0][0][:, 0:cols], ts_[0][0][:, 0:cols], ts_[1][0][:, 0:cols], ALU.add, [ts_[0][1], ts_[1][1]], [ts_[0][1]])
                TT("pool", merged[:, m, 0:cols], ts_[0][0][:, 0:cols], ts_[2][0][:, 0:cols], ALU.add, [ts_[0][1], ts_[2][1]], [merged_b[m]])

            wo_l = []
            for n in range(2):
                wo_, wo_b = wget("wo%d" % n, hold=n)
                wo_l.append((wo_[:, :].rearrange("p (k c) -> p k c", k=8), wo_b))
            r2s = []
            for t in range(NT):
                for n in range(2):
                    wo3, wo_b = wo_l[n]
                    bk, bk_b = pb()
                    for k in range(8):
                        MM(bk[:, :], merged[:, k, t * 128:(t + 1) * 128], wo3[:, k, :], k == 0, k == 7, [merged_b[k], wo_b], [bk_b])
                    TT("dve", xres[t][:, n * 512:(n + 1) * 512], bk[:, :], xres[t][:, n * 512:(n + 1) * 512], ALU.add,
                       [bk_b, xres_b[t]], [xres_b[t]])
                    if n == 0 and t >= 1:
                        norm_b(xres[t - 1], xres_b[t - 1], 1, hT, [[hT_bb[t - 1][0]], [hT_bb[t - 1][1]]], (t - 1) * 128)
                r2s.append(rms_rstd(xres[t][:], [xres_b[t]], 1024, power=-1.0))
            def ffn_evac(f, bk, bk_b):
                rl, rl_b = t512()
                ACT(rl[:, 0:cols], bk[:, 0:cols], AF.Relu, [bk_b], [rl_b])
                TT("pool" if (f % 2) else "dve", Rt[:, f * rw: f * rw + cols], rl[:, 0:cols], rl[:, 0:cols], ALU.mult, [rl_b], [Rt_b[f]])

            def sample_hooks(c):
                if nxt is not None and nxt[0] == "sample":
                    if c == 0:
                        sample_hist_prep()
                        sample_consts_ops()
                    sample_kv_prep(2 * c)
                    sample_kv_prep(2 * c + 1)

            split = NT == 4
            if split:
                sample_hooks(0)
                wu, wu_b = wget("wup0")
                wu3 = wu[:, :].rearrange("p (k c) -> p k c", k=8)
                bk0 = []

                def a_group(mt):
                    bk, bk_b = pb()
                    bk0.append((bk, bk_b))
                    for k in range(8):
                        MM(bk[:, 0:384], wu3[:, k, mt * 128:(mt + 1) * 128], hT[:, k, 0:384], k == 0, k == 7,
                           hT_all[0:6] + [wu_b], [bk_b])

                a_group(0)
                a_group(1)
            norm_b(xres[NT - 1], xres_b[NT - 1], 1, hT, [[hT_bb[NT - 1][0]], [hT_bb[NT - 1][1]]], (NT - 1) * 128)
            if split:
                a_group(2)
                a_group(3)
                for mt in range(4):
                    bk, bk_b = bk0[mt]
                    for k in range(8):
                        MM(bk[:, 384:512], wu3[:, k, mt * 128:(mt + 1) * 128], hT[:, k, 384:512], k == 0, k == 7,
                           hT_bb[3] + [wu_b], [bk_b])
                    ffn_evac(mt, bk, bk_b)

            nxt_xn = P0a(*nxt) if nxt is not None else None

            for c in range(8):
                if split and c == 0:
                    continue
                sample_hooks(c)
                wu, wu_b = wget("wup%d" % c)
                wu3 = wu[:, :].rearrange("p (k c) -> p k c", k=8)
                for mt in range(4):
                    f = c * 4 + mt
                    bk, bk_b = pb()
                    for k in range(8):
                        MM(bk[:, 0:cols], wu3[:, k, mt * 128:(mt + 1) * 128], hT[:, k, 0:cols], k == 0, k == 7, hT_all + [wu_b], [bk_b])
                    ffn_evac(f, bk, bk_b)

            if nxt is not None:
                P0b(nxt[0], nxt[1], nxt_xn)

            for kc in range(4):
                for n in range(2):
                    wd, wd_b = wget("wdn%d_%d" % (kc, n))
                    wd3 = wd[:, :].rearrange("p (k c) -> p k c", k=8)
                    for t in range(NT):
                        bi = 2 * t + n
                        for k in range(8):
                            f = kc * 8 + k
                            MM(banks[bi][:, :], Rt[:, f * rw + t * 128: f * rw + (t + 1) * 128], wd3[:, k, :],
                               kc == 0 and k == 0, kc == 3 and k == 7, [Rt_b[f], wd_b], [bank_b[bi]])
            rr["bank"] = 0
            for t in range(NT):
                for n in range(2):
                    bi = 2 * t + n
                    STT("dve", xres[t][:, n * 512:(n + 1) * 512], banks[bi][:, :], r2s[t][0], xres[t][:, n * 512:(n + 1) * 512],
                        ALU.mult, ALU.add, [bank_b[bi], xres_b[t], r2s[t][1]], [xres_b[t]])
                rstd, rb = rms_rstd(xres[t][:], [xres_b[t]], 1024)
                yt, yt_b = t1024()
                STT("dve", yt[:], xres[t][:], rstd, gfb[:], ALU.mult, ALU.mult, [xres_b[t], rb, cst_b], [yt_b])
                dst = ys if sample else yp[j * 512 + t * 128: j * 512 + (t + 1) * 128, :]
                s.dma("sp", [(dst, yt[:])], reads=[yt_b], owner=yt_b)
            route["pool2dve"] = False

        order = [("prompt", j) for j in range(4)] + [("sample", 0)]
        first_xn = P0a(*order[0])
        mem_prep_early()
        P0b(order[0][0], order[0][1], first_xn)
        for i, (kind, j) in enumerate(order):
            run_main(kind, j, order[i + 1] if i + 1 < len(order) else None)
            if i == 0:
                sample_consts()
        assert ws["pos"] == len(stream)
        s.emit()
    return nc


_NC = None


def kernel(x_prompt, x_sample, mem_prompt, cache_mem_k, cache_mem_v, state_pool,
           g_mix, w_in, g_v, b_v, w_s, b_s, w_pool, pool_scale, g_mem, w_kv,
           w_out_a, w_out_b, w_out_c, w_o, g_ffn, w_up, w_down, g_final):
    global _NC
    if _NC is None:
        _NC = build_program()
    nc = _NC
    f = lambda a: np.ascontiguousarray(np.asarray(a, dtype=np.float32))
    shared = {
        "g_mix": f(g_mix[0]), "w_in": f(w_in[0]), "g_v": f(g_v[0]), "b_v": f(b_v[0]), "w_s": f(w_s[0]),
        "b_s": f(b_s[0]), "w_pool": f(w_pool[0]), "pool_scale": f(pool_scale[0]), "g_mem": f(g_mem[0]),
        "w_kv": f(w_kv[0]), "w_out_a": f(w_out_a[0]), "w_out_b": f(w_out_b[0]), "w_out_c": f(w_out_c[0]),
        "w_o": f(w_o[0]), "g_ffn": f(g_ffn[0]), "w_up": f(w_up[0]), "w_down": f(w_down[0]), "g_final": f(g_final),
    }
    in_maps = []
    for c in range(8):
        d = dict(shared)
        d["xp"] = f(x_prompt[c])
        d["xs"] = f(np.asarray(x_sample)[16 * c:16 * (c + 1)].reshape(128, 1024))
        d["mem"] = f(mem_prompt[c])
        d["ck"] = f(np.asarray(cache_mem_k)[0, 16 * c:16 * (c + 1)].reshape(16, 256, 256))
        d["cv"] = f(np.asarray(cache_mem_v)[0, 16 * c:16 * (c + 1)].reshape(16, 256, 256))
        d["spst"] = f(np.asarray(state_pool)[0, 16 * c:16 * (c + 1)])
        in_maps.append(d)
    res = run_bass_kernel_spmd(nc, in_maps, core_ids=list(range(8)))
    r = res.results
    y_prompt = np.stack([r[c]["yp"] for c in range(8)], 0).astype(np.float32)
    y_sample = np.concatenate([r[c]["ys"].reshape(16, 8, 1024) for c in range(8)], 0).astype(np.float32)
    mem_k = np.stack([r[c]["mk"].reshape(256, 4, 64) for c in range(8)], 0)[None].astype(np.float32)
    mem_v = np.stack([r[c]["mv"].reshape(256, 4, 64) for c in range(8)], 0)[None].astype(np.float32)
    pool_p = np.stack([r[c]["pool_p"] for c in range(8)], 0)[None].astype(np.float32)
    pool_s = np.concatenate([r[c]["pool_s"] for c in range(8)], 0)[None].astype(np.float32)
    cv_p = np.stack([r[c]["cv_p"] for c in range(8)], 0)[None].astype(np.float32)
    cv_s = np.concatenate([r[c]["cv_s"].reshape(16, 8, 512) for c in range(8)], 0)[None].astype(np.float32)
    return (y_prompt, y_sample, mem_k, mem_v, pool_p, pool_s, cv_p, cv_s)
```

```python
import contextlib
import numpy as np
import concourse.bass as bass
import concourse.mybir as mybir
from concourse.bass_utils import run_bass_kernel_spmd

F32 = mybir.dt.float32
BF16 = mybir.dt.bfloat16
AF = mybir.ActivationFunctionType
ALU = mybir.AluOpType

ENGS = ("pe", "act", "dve", "pool", "sp")
EPS = 1e-6
NSLOT = 5
NF512 = 12
NF1024 = 4


class Buf:
    __slots__ = ("name", "last_w", "readers", "sem", "cnt")

    def __init__(self, name):
        self.name = name
        self.last_w = None
        self.readers = []
        self.sem = None
        self.cnt = 0


class _Op:
    __slots__ = ("eng", "fn", "deps", "signal", "sigcount", "dma", "idx")


class Sched:
    def __init__(self, nc):
        self.nc = nc
        self.ops = {e: [] for e in ENGS}
        self.owners = []

    def buf(self, name):
        return Buf(name)

    @staticmethod
    def _flat(x):
        out = []
        for b in x:
            if isinstance(b, (list, tuple)):
                out.extend(Sched._flat(b))
            else:
                out.append(b)
        return out

    def _register(self, eng, fn, reads, writes, dma):
        reads = self._flat(reads)
        writes = self._flat(writes)
        op = _Op()
        op.eng, op.fn, op.dma = eng, fn, dma
        op.signal = False
        op.idx = len(self.ops[eng])
        deps = []
        for b in list(reads) + list(writes):
            if b.last_w is not None:
                deps.append(b.last_w)
        for b in writes:
            deps.extend(b.readers)
        op.deps = deps
        if dma is not None:
            owner, n = dma
            if owner not in self.owners:
                self.owners.append(owner)
            owner.cnt += 16 * n
            tok = ("dma", owner, owner.cnt)
        else:
            tok = ("eng", op)
        for b in reads:
            b.readers.append(tok)
        for b in writes:
            b.last_w = tok
            b.readers = []
        self.ops[eng].append(op)
        return op

    def op(self, eng, fn, reads=(), writes=()):
        return self._register(eng, fn, reads, writes, None)

    def dma(self, queue, pairs, reads=(), writes=(), owner=None):
        assert owner is not None
        return self._register(queue, pairs, reads, writes, (owner, len(pairs)))

    def emit(self, final_wait_eng="sp"):
        nc = self.nc
        for e in ENGS:
            for op in self.ops[e]:
                for d in op.deps:
                    if d[0] == "eng":
                        p = d[1]
                        if p.eng == "pe" and op.eng == "pe" and op.dma is None:
                            continue
                        p.signal = True
        for e in ENGS:
            c = 0
            for op in self.ops[e]:
                if op.dma is None and op.signal:
                    c += 1
                    op.sigcount = c
        with contextlib.ExitStack() as st:
            esem = {e: st.enter_context(nc.semaphore("s_" + e)) for e in ENGS}
            for i, o in enumerate(self.owners):
                o.sem = st.enter_context(nc.semaphore("d%d" % i))
            block = st.enter_context(nc.Block())
            owners = self.owners

            def run(e, eng):
                seen = {}
                for op in self.ops[e]:
                    waits = {}
                    for d in op.deps:
                        if d[0] == "eng":
                            p = d[1]
                            if p.eng == "pe" and e == "pe" and op.dma is None:
                                continue
                            key, val = esem[p.eng], p.sigcount
                        else:
                            key, val = d[1].sem, d[2]
                        if val > waits.get(key, 0):
                            waits[key] = val
                    for key, val in waits.items():
                        if val > seen.get(key, 0):
                            eng.wait_ge(key, val)
                            seen[key] = val
                    if op.dma is not None:
                        owner, n = op.dma
                        for (o_ap, i_ap) in op.fn:
                            eng.dma_start(out=o_ap, in_=i_ap).then_inc(owner.sem, 16)
                    else:
                        ins = op.fn(eng)
                        if op.signal:
                            ins.then_inc(esem[e], 1)
                if e == final_wait_eng:
                    for o in owners:
                        if o.cnt > seen.get(o.sem, 0):
                            eng.wait_ge(o.sem, o.cnt)

            @block.tensor
            def _(eng):
                run("pe", eng)

            @block.scalar
            def _(eng):
                run("act", eng)

            @block.vector
            def _(eng):
                run("dve", eng)

            @block.gpsimd
            def _(eng):
                run("pool", eng)

            @block.sync
            def _(eng):
                run("sp", eng)


def build_program():
    nc = bass.Bass("TRN2", target_bir_lowering=False)

    def din(name, shape, dt=F32):
        return nc.dram_tensor(name, list(shape), dt, kind="ExternalInput").ap()

    def dout(name, shape):
        return nc.dram_tensor(name, list(shape), F32, kind="ExternalOutput").ap()

    def dscr(name, shape):
        return nc.dram_tensor(name, list(shape), BF16, kind="Internal").ap()

    xp = din("xp", [2048, 1024])
    xs = din("xs", [128, 1024])
    mem = din("mem", [256, 1024])
    ck = din("ck", [16, 256, 256])
    cv = din("cv", [16, 256, 256])
    spst = din("spst", [16, 15, 256])
    g_mix = din("g_mix", [1024])
    w_in = din("w_in", [1024, 4608])
    g_v = din("g_v", [512])
    b_v = din("b_v", [512])
    w_s = din("w_s", [4, 128, 128])
    b_s = din("b_s", [4, 128])
    w_pool = din("w_pool", [4, 64, 64])
    pool_scale = din("pool_scale", [256])
    g_mem = din("g_mem", [1024])
    w_kv = din("w_kv", [1024, 512])
    w_out_a = din("w_out_a", [512, 1024])
    w_out_b = din("w_out_b", [256, 1024])
    w_out_c = din("w_out_c", [256, 1024])
    w_o = din("w_o", [1024, 1024])
    g_ffn = din("g_ffn", [1024])
    w_up = din("w_up", [1024, 4096])
    w_down = din("w_down", [4096, 1024])
    g_final = din("g_final", [1024])

    yp = dout("yp", [2048, 1024])
    ys = dout("ys", [128, 1024])
    mk = dout("mk", [256, 256])
    mv = dout("mv", [256, 256])
    pool_p = dout("pool_p", [15, 256])
    pool_s = dout("pool_s", [16, 15, 256])
    cv_p = dout("cv_p", [128, 512])
    cv_s = dout("cv_s", [128, 512])

    s_all = dscr("s_all", [29, 128, 4096])

    st = contextlib.ExitStack()
    with st, nc.allow_non_contiguous_dma(reason="small strided constant loads"), \
            nc.allow_low_precision(reason="bf16 matmul operands, fp32 accumulate"):
        s = Sched(nc)

        def sb(name, shape, dt):
            return st.enter_context(nc.sbuf_tensor(name, list(shape), dt))

        xres = [sb("xres%d" % t, [128, 1024], F32) for t in range(4)]
        xres_b = [s.buf("xres%d" % t) for t in range(4)]
        hT = sb("hT", [128, 8, 512], BF16)
        hT_bb = [[s.buf("hT%d_%d" % (t, h)) for h in range(2)] for t in range(4)]
        pT = sb("pT", [128, 2, 528], F32)
        pT_b = s.buf("pT")
        pTs = sb("pTs", [128, 2, 368], F32)
        pTs_b = s.buf("pTs")
        arena = sb("arena", [128, 16384], BF16)
        regB = [s.buf("reg%d" % i) for i in range(32)]

        def aview(r0, nreg, k):
            return arena[:, r0 * 512:(r0 + nreg) * 512].rearrange("p (k c) -> p k c", k=k)

        CTX = {
            "prompt": dict(
                merged=aview(0, 8, 8), merged_b=[regB[m] for m in range(8)],
                probs=aview(8, 8, 8), probs_b=[[regB[8 + 4 * hp + i] for i in range(4)] for hp in range(2)],
                apre=aview(16, 4, 4), apre_b=[regB[16 + g] for g in range(4)],
                vh16=aview(20, 4, 4), vh16_b=[regB[20 + t] for t in range(4)],
                pooled=aview(24, 2, 2), pooled_b=[regB[24], regB[25]],
                pm=aview(26, 2, 2), pm_b=[regB[26], regB[27]],
                qT=aview(28, 2, 2), qT_b=[regB[28], regB[29]],
                oT=aview(30, 2, 2), oT_b=[regB[30], regB[31]]),
            "sample": dict(
                merged=sb("merged_s", [128, 8, 128], BF16), merged_b=[s.buf("merged_s%d" % m) for m in range(8)],
                probs=sb("probs_s", [128, 8, 128], BF16), probs_b=[s.buf("probs_s%d" % i) for i in range(2)],
                apre=sb("apre_s", [128, 4, 128], BF16), apre_b=[s.buf("apre_s%d" % g) for g in range(4)],
                vh16=sb("vh16_s", [128, 1, 512], BF16), vh16_b=[s.buf("vh16_s")],
                pooled=sb("pooled_s", [128, 2, 128], BF16), pooled_b=s.buf("pooled_s"),
                pm=sb("pm_s", [128, 2, 128], BF16), pm_b=s.buf("pm_s"),
                qT=sb("qT_s", [128, 2, 128], BF16), qT_b=s.buf("qT_s"),
                oT=sb("oT_s", [128, 2, 128], BF16), oT_b=[s.buf("oT_s%d" % i) for i in range(2)]),
        }
        merged = CTX["prompt"]["merged"]
        merged_b = CTX["prompt"]["merged_b"]
        R = sb("R", [128, 32 * 512], BF16)
        R_b = [s.buf("R%d" % i) for i in range(32)]
        junk = sb("junk", [128, 1024], BF16)
        stats = sb("stats", [128, 512], F32)
        stats_init = s.buf("stats_init")
        slots = [sb("slot%d" % i, [128, 4096], BF16) for i in range(NSLOT)]
        slot_b = [s.buf("slot%d" % i) for i in range(NSLOT)]
        f512 = [sb("f512_%d" % i, [128, 512], F32) for i in range(NF512)]
        f512_b = [s.buf("f512_%d" % i) for i in range(NF512)]
        f1024 = [sb("f1024_%d" % i, [128, 1024], F32) for i in range(NF1024)]
        f1024_b = [s.buf("f1024_%d" % i) for i in range(NF1024)]
        ident = sb("ident", [128, 128], F32)
        maskT = sb("maskT", [128, 128], F32)
        wsT_p = sb("wsT_p", [128, 4, 128], BF16)
        wsT_s = sb("wsT_s", [128, 4, 128], BF16)
        bs_p = sb("bs_p", [128, 4, 128], F32)
        bs_s = sb("bs_s", [128, 4, 128], F32)
        gvb = sb("gvb", [128, 512], F32)
        bvb = sb("bvb", [128, 512], F32)
        gfb = sb("gfb", [128, 1024], F32)
        gcols = sb("gcols", [128, 3, 8], F32)
        pscol = sb("pscol", [128, 2], F32)
        invw = sb("invw", [128, 2], F32)
        wcol = sb("wcol", [128, 2], F32)
        invcnt = sb("invcnt", [128, 2, 16], F32)
        wpbd = sb("wpbd", [128, 2, 128], BF16)
        kT_p = sb("kT_p", [128, 2, 256], BF16)
        v_p = sb("v_p", [128, 2, 256], BF16)
        ones = sb("ones", [128, 64], BF16)
        cst_b = s.buf("consts")
        cst2_b = s.buf("consts2")

        banks = [st.enter_context(nc.psum_tensor("bank%d" % i, [128, 512], F32)) for i in range(8)]
        bank_b = [s.buf("bank%d" % i) for i in range(8)]

        rr = {"bank": 0, "f512": 0, "f1024": 0, "stat": 0, "alt": 0}

        def pb():
            i = rr["bank"]
            rr["bank"] = (i + 1) % 8
            return banks[i], bank_b[i]

        def t512():
            i = rr["f512"]
            rr["f512"] = (i + 1) % NF512
            return f512[i], f512_b[i]

        def t1024():
            i = rr["f1024"]
            rr["f1024"] = (i + 1) % NF1024
            return f1024[i], f1024_b[i]

        def statcols(n):
            i = rr["stat"]
            rr["stat"] = i + n
            assert rr["stat"] <= 512
            return i

        def alt():
            rr["alt"] ^= 1
            return "act" if rr["alt"] else "dve"

        def MM(out, lhsT, rhs, start, stop, R_, W_):
            s.op("pe", lambda e: e.matmul(out, lhsT=lhsT, rhs=rhs, start=start, stop=stop), R_, W_)

        def TR(out, in_, idn, R_, W_):
            s.op("pe", lambda e: e.transpose(out=out, in_=in_, identity=idn), R_, W_)

        def ACT(out, in_, func, R_, W_, **kw):
            s.op("act", lambda e: e.activation(out=out, in_=in_, func=func, **kw), R_, W_)

        route = {"pool2dve": False}

        def rt(eng):
            return "dve" if (eng == "pool" and route["pool2dve"]) else eng

        def TT(eng, out, in0, in1, op, R_, W_):
            eng = rt(eng)
            s.op(eng, lambda e: e.tensor_tensor(out=out, in0=in0, in1=in1, op=op), R_, W_)

        def TS(eng, out, in0, s1, s2, op0, op1, R_, W_):
            eng = rt(eng)
            if s2 is None:
                s.op(eng, lambda e: e.tensor_scalar(out=out, in0=in0, scalar1=s1, scalar2=None, op0=op0), R_, W_)
            else:
                s.op(eng, lambda e: e.tensor_scalar(out=out, in0=in0, scalar1=s1, scalar2=s2, op0=op0, op1=op1), R_, W_)

        def STT(eng, out, in0, scalar, in1, op0, op1, R_, W_):
            eng = rt(eng)
            s.op(eng, lambda e: e.scalar_tensor_tensor(out=out, in0=in0, scalar=scalar, in1=in1, op0=op0, op1=op1), R_, W_)

        def CP(eng, out, in_, R_, W_):
            eng = rt(eng)
            if eng == "act":
                s.op("act", lambda e: e.activation(out=out, in_=in_, func=AF.Identity), R_, W_)
            else:
                s.op(eng, lambda e: e.tensor_copy(out=out, in_=in_), R_, W_)

        def MEMSET(eng, ap, val, W_):
            eng = rt(eng)
            s.op(eng, lambda e: e.memset(ap, val), (), W_)

        def kview(ap, k):
            return ap.rearrange("(k p) c -> p k c", p=128)

        def chunk_pairs(name, dst):
            d8 = dst.rearrange("p (k c) -> p k c", k=8)
            if name == "wkv":
                return [(d8, kview(w_kv, 8))]
            if name.startswith("win"):
                c = int(name[3:])
                return [(d8, kview(w_in[:, c * 512:(c + 1) * 512], 8))]
            if name.startswith("mrg"):
                m = int(name[3:])
                prs = []
                gv_ = dst[:, 0:3072].rearrange("p (k b c) -> p k b c", k=8, b=3)
                for br in range(3):
                    c0 = 1536 + br * 1024 + m * 128
                    prs.append((gv_[:, :, br, :], kview(w_in[:, c0:c0 + 128], 8)))
                prs.append((dst[:, 3072:3584].rearrange("p (k c) -> p k c", k=4), kview(w_out_a[:, m * 128:(m + 1) * 128], 4)))
                prs.append((dst[:, 3584:3840].rearrange("p (k c) -> p k c", k=2), kview(w_out_b[:, m * 128:(m + 1) * 128], 2)))
                prs.append((dst[:, 3840:4096].rearrange("p (k c) -> p k c", k=2), kview(w_out_c[:, m * 128:(m + 1) * 128], 2)))
                return prs
            if name.startswith("wo"):
                n = int(name[2:])
                return [(d8, kview(w_o[:, n * 512:(n + 1) * 512], 8))]
            if name.startswith("wup"):
                c = int(name[3:])
                return [(d8, kview(w_up[:, c * 512:(c + 1) * 512], 8))]
            kc, n = name[3:].split("_")
            kc, n = int(kc), int(n)
            return [(d8, kview(w_down[kc * 1024:(kc + 1) * 1024, n * 512:(n + 1) * 512], 8))]

        block_chunks = (["win1", "win0", "win2"] + ["mrg%d" % m for m in range(8)] + ["wo0", "wo1"] +
                        ["wup%d" % c for c in range(8)] + ["wdn%d_%d" % (kc, n) for kc in range(4) for n in range(2)])
        stream = block_chunks * 5
        stream.insert(3, "wkv")
        sinfo = []
        _n = 0
        for nm_ in stream:
            if nm_ == "wkv":
                sinfo.append((0, -1))
            else:
                sinfo.append((_n // len(block_chunks), _n % len(block_chunks)))
                _n += 1
        chunk_idx = {nm: i for i, nm in enumerate(block_chunks)}
        scr = {nm: s.buf("scr_" + nm) for nm in block_chunks}
        ws = {"next_load": 0, "pos": 0, "wb": None}
        slot_sw = [s.buf("slot_sw%d" % i) for i in range(NSLOT)]

        def wget(expect, hold=0):
            i = ws["pos"]
            assert stream[i] == expect, (stream[i], expect)
            ws["pos"] = i + 1
            while ws["next_load"] < len(stream) and ws["next_load"] <= i + NSLOT - 1 - hold:
                j = ws["next_load"]
                nm = stream[j]
                sl = j % NSLOT
                blk_i, ci = sinfo[j]
                do_cast = (ci < 0) or (blk_i == 0) or (blk_i == 1 and ci % 2 == 1)
                do_wb = (ci >= 0) and ((blk_i == 0 and ci % 2 == 0) or (blk_i == 1 and ci % 2 == 1))
                if do_cast:
                    s.dma("pool", chunk_pairs(nm, slots[sl][:, :]), writes=[slot_b[sl]], owner=slot_sw[sl])
                    if ws["wb"] is not None:
                        pn, psl = ws["wb"]
                        s.dma("pool", [(s_all[chunk_idx[pn]], slots[psl][:, :])], reads=[slot_b[psl]], writes=[scr[pn]], owner=scr[pn])
                    ws["wb"] = (nm, sl) if do_wb else None
                else:
                    if ws["wb"] is not None:
                        pn, psl = ws["wb"]
                        s.dma("pool", [(s_all[chunk_idx[pn]], slots[psl][:, :])], reads=[slot_b[psl]], writes=[scr[pn]], owner=scr[pn])
                        ws["wb"] = None
                    s.dma("sp", [(slots[sl][:, :], s_all[chunk_idx[nm]])], reads=[scr[nm]], writes=[slot_b[sl]], owner=slot_b[sl])
                ws["next_load"] = j + 1
            sl = i % NSLOT
            return slots[sl], slot_b[sl]

        def pbc(ap1d, n):
            return ap1d.partition_broadcast(128)

        cpairs = [
            (bs_p[:], b_s.partition_broadcast(128)),
            (gvb[:], g_v.partition_broadcast(128)),
            (bvb[:], b_v.partition_broadcast(128)),
            (gfb[:], g_final.partition_broadcast(128)),
        ]
        s.dma("sp", cpairs, writes=[cst_b], owner=cst_b)

        ident_b = s.buf("ident")
        mask_b = s.buf("mask")
        MEMSET("pool", ident[:], 1.0, [ident_b])
        s.op("pool", lambda e: e.affine_select(out=ident[:], in_=ident[:], compare_op=ALU.is_equal, fill=0.0, base=0,
                                               pattern=[[-1, 128]], channel_multiplier=1), [ident_b], [ident_b])
        MEMSET("pool", maskT[:], 1.0, [mask_b])
        s.op("pool", lambda e: e.affine_select(out=maskT[:], in_=maskT[:], compare_op=ALU.is_ge, fill=0.0, base=0,
                                               pattern=[[1, 128]], channel_multiplier=-1), [mask_b], [mask_b])

        def mk_small(e):
            e.memset(stats[:], 0.0)
            e.memset(ones[:], 1.0)
            e.memset(invw[0:64, 0:1], 0.5)
            e.memset(invw[64:128, 0:1], 0.25)
            e.memset(invw[0:64, 1:2], 0.125)
            e.memset(invw[64:128, 1:2], 0.0625)
            e.memset(wcol[0:64, 0:1], 2.0)
            e.memset(wcol[64:128, 0:1], 4.0)
            e.memset(wcol[0:64, 1:2], 8.0)
            e.memset(wcol[64:128, 1:2], 16.0)
            ins = None
            for t in range(16):
                ins = e.memset(invcnt[:, :, t:t + 1], float(t + 1))
            return ins

        s.op("pool", mk_small, (), [cst2_b, stats_init])
        for ft in range(2):
            TS("pool", invcnt[:, ft, :], invcnt[:, ft, :], wcol[:, ft:ft + 1], None, ALU.min, None, [cst2_b], [cst2_b])
        s.op("dve", lambda e: e.reciprocal(out=invcnt[:], in_=invcnt[:]), [cst2_b], [cst2_b])

        gst, gst_b = t512()
        s.dma("sp", [(gst[0:8, 0:128], g_mix.rearrange("(k p) -> k p", p=128)),
                     (gst[0:8, 128:256], g_ffn.rearrange("(k p) -> k p", p=128)),
                     (gst[0:8, 256:384], g_mem.rearrange("(k p) -> k p", p=128)),
                     (gst[0:2, 384:512], pool_scale.rearrange("(k p) -> k p", p=128))], writes=[gst_b], owner=gst_b)
        gcol_b = s.buf("gcol")
        bk, bk_b = pb()
        for i in range(3):
            TR(bk[:, i * 8:(i + 1) * 8], gst[0:8, i * 128:(i + 1) * 128], ident[0:8, 0:8], [gst_b, ident_b], [bk_b])
        TR(bk[:, 24:26], gst[0:2, 384:512], ident[0:2, 0:2], [gst_b, ident_b], [bk_b])
        CP("dve", gcols[:].rearrange("p a b -> p (a b)"), bk[:, 0:24], [bk_b], [gcol_b])
        CP("dve", pscol[:], bk[:, 24:26], [bk_b], [gcol_b])

        wpst, wpst_b = t512()
        MEMSET("pool", wpst[:, 0:256], 0.0, [wpst_b])
        wpv = wpst[:, 0:256].rearrange("p (t c) -> p t c", t=2)
        s.dma("sp", [(wpv[(g % 2) * 64:(g % 2) * 64 + 64, g // 2, (g % 2) * 64:(g % 2) * 64 + 64], w_pool[g]) for g in range(4)],
              writes=[wpst_b], owner=wpst_b)
        wpbd_b = s.buf("wpbd")
        CP("pool", wpbd[:], wpv, [wpst_b], [wpbd_b])

        wsn, wsn_b = t512()
        s.dma("sp", [(wsn[:].rearrange("p (g j) -> p g j", g=4), w_s.rearrange("g i j -> i g j"))], writes=[wsn_b], owner=wsn_b)
        bk, bk_b = pb()
        for g in range(4):
            TR(bk[:, g * 128:(g + 1) * 128], wsn[:, g * 128:(g + 1) * 128], ident[:], [wsn_b, ident_b], [bk_b])
        wsT_b = s.buf("wsT")
        wsTs_b = s.buf("wsTs")
        for g in range(4):
            TT("dve", wsT_p[:, g, :], bk[:, g * 128:(g + 1) * 128], maskT[:], ALU.mult, [bk_b, mask_b], [wsT_b])

        csts_b = s.buf("consts_s")
        wss = bs_s_stage = sb("wss", [128, 512], F32)
        wss_b = s.buf("wss")
        wssv = wss[:].rearrange("p (g i) -> p g i", g=4)

        def sample_consts():
            s.dma("sp", [(wssv[0:8, g, 0:8], w_s[g, 0:8, 0:8].rearrange("i j -> j i")) for g in range(4)],
                  writes=[wss_b], owner=wss_b)

        def sample_consts_ops():
            DBL = (8, 16, 32, 64)
            for w in DBL:
                CP("pool", wssv[0:8, :, w:2 * w], wssv[0:8, :, 0:w], [wss_b], [wss_b])
            Lt, Lt_b = t512()
            CP("pool", Lt[0:8, 0:8], ident[0:8, 0:8], [ident_b], [Lt_b])
            for w in DBL:
                CP("pool", Lt[0:8, w:2 * w], Lt[0:8, 0:w], [Lt_b], [Lt_b])
            Et, Et_b = t512()
            MEMSET("pool", Et[0:16, 0:128], 1.0, [Et_b])
            s.op("pool", lambda e: e.affine_select(out=Et[0:16, 0:128], in_=Et[0:16, 0:128], compare_op=ALU.is_ge, fill=0.0,
                                                   base=0, pattern=[[1, 128]], channel_multiplier=-8), [Et_b], [Et_b])
            s.op("pool", lambda e: e.affine_select(out=Et[0:16, 0:128], in_=Et[0:16, 0:128], compare_op=ALU.is_ge, fill=0.0,
                                                   base=7, pattern=[[-1, 128]], channel_multiplier=8), [Et_b], [Et_b])
            bk, bk_b = pb()
            MM(bk[:, 0:128], Et[0:16, 0:128], Et[0:16, 0:128], True, True, [Et_b], [bk_b])
            M2, M2_b = t512()
            TT("dve", M2[:, 0:128], bk[:, 0:128], maskT[:], ALU.mult, [bk_b, mask_b], [M2_b])
            bk2, bk2_b = pb()
            for g in range(4):
                MM(bk2[:, g * 128:(g + 1) * 128], Lt[0:8, 0:128], wssv[0:8, g, :], True, True, [Lt_b, wss_b], [bk2_b])
            for g in range(4):
                TT("dve", wsT_s[:, g, :], bk2[:, g * 128:(g + 1) * 128], M2[:, 0:128], ALU.mult, [bk2_b, M2_b], [wsTs_b])
            CP("pool", bs_s[:, :, 0:8], bs_p[:, :, 0:8], [cst_b], [csts_b])
            for w in DBL:
                CP("pool", bs_s[:, :, w:2 * w], bs_s[:, :, 0:w], [csts_b], [csts_b])

        def rms_rstd(x_ap, x_bufs, width, power=-0.5):
            c = statcols(4)
            sbuf_ = s.buf("st%d" % c)
            ACT(junk[:, 0:width], x_ap, AF.Square, list(x_bufs) + [stats_init], [sbuf_], accum_out=stats[:, c:c + 1])
            TS("dve", stats[:, c + 1:c + 2], stats[:, c:c + 1], 1.0 / width, EPS, ALU.mult, ALU.add, [sbuf_], [sbuf_])
            ACT(stats[:, c + 2:c + 3], stats[:, c + 1:c + 2], AF.Ln, [sbuf_], [sbuf_])
            ACT(stats[:, c + 3:c + 4], stats[:, c + 2:c + 3], AF.Exp, [sbuf_], [sbuf_], scale=power)
            return stats[:, c + 3:c + 4], sbuf_

        def norm_a(x_ap, x_buf, inplace):
            rstd, rb = rms_rstd(x_ap, [x_buf], 1024)
            if inplace is not None:
                xn, xn_b = inplace
            else:
                xn, xn_b = t1024()
            TS("dve", xn[:], x_ap, rstd, None, ALU.mult, None, [x_buf, rb], [xn_b])
            return xn, xn_b

        def norm_b(xn, xn_b, gidx, dstT, dst_bufs, col0):
            for half in range(2):
                bk, bk_b = pb()
                for kk in range(4):
                    k = half * 4 + kk
                    TR(bk[:, kk * 128:(kk + 1) * 128], xn[:, k * 128:(k + 1) * 128], ident[:], [xn_b, ident_b], [bk_b])
                for kk in range(4):
                    k = half * 4 + kk
                    gc = gcols[:, gidx, k:k + 1]
                    o = dstT[:, k, col0:col0 + 128]
                    i_ = bk[:, kk * 128:(kk + 1) * 128]
                    if half == 0:
                        ACT(o, i_, AF.Identity, [bk_b, gcol_b], dst_bufs[half], scale=gc)
                    else:
                        TS("dve", o, i_, gc, None, ALU.mult, None, [bk_b, gcol_b], dst_bufs[half])

        def norm_T(x_ap, x_buf, gidx, dstT, dst_bufs, col0):
            xn, xn_b = norm_a(x_ap, x_buf, None)
            norm_b(xn, xn_b, gidx, dstT, dst_bufs, col0)

        hmT = merged
        hm_l = list(merged_b)
        kvp_b = s.buf("kvp")
        mem_xn = []

        def mem_prep_early():
            for t in range(2):
                s.dma("sp", [(xres[2 + t][:], mem[t * 128:(t + 1) * 128, :])], writes=[xres_b[2 + t]], owner=xres_b[2 + t])
                mem_xn.append(norm_a(xres[2 + t][:], xres_b[2 + t], (xres[2 + t], xres_b[2 + t])))

        def mem_prep_pe():
            for t in range(2):
                norm_b(mem_xn[t][0], mem_xn[t][1], 2, hmT, [hm_l, hm_l], t * 128)
            wk, wk_b = wget("wkv")
            wkv3 = wk[:, :].rearrange("p (k c) -> p k c", k=8)
            for t in range(2):
                bk, bk_b = pb()
                for k in range(8):
                    MM(bk[:, :], hmT[:, k, t * 128:(t + 1) * 128], wkv3[:, k, :], k == 0, k == 7, hm_l + [wk_b], [bk_b])
                kvf, kvf_b = t512()
                CP("dve", kvf[:], bk[:, :], [bk_b], [kvf_b])
                CP("pool", v_p[:, t, :], kvf[:, 256:512], [kvf_b], [kvp_b])
                s.dma("sp", [(mk[t * 128:(t + 1) * 128, :], kvf[:, 0:256]), (mv[t * 128:(t + 1) * 128, :], kvf[:, 256:512])],
                      reads=[kvf_b], owner=kvf_b)
            for hp in range(2):
                bk, bk_b = pb()
                for k in range(8):
                    MM(bk[:, 0:256], wkv3[:, k, hp * 128:(hp + 1) * 128], hmT[:, k, 0:256], k == 0, k == 7, hm_l + [wk_b], [bk_b])
                CP("dve", kT_p[:, hp, :], bk[:, 0:256], [bk_b], [kvp_b])

        def sample_kv_prep(b):
            t1, t1_b = t512()
            s.dma("sp", [(t1[:].rearrange("p (c f) -> p c f", c=2), ck[b].rearrange("(c m) f -> m c f", m=128))],
                  writes=[t1_b], owner=t1_b)
            bk, bk_b = pb()
            for c in range(2):
                for hp in range(2):
                    TR(bk[:, hp * 256 + c * 128: hp * 256 + (c + 1) * 128], t1[:, c * 256 + hp * 128: c * 256 + (hp + 1) * 128],
                       ident[:], [t1_b, ident_b], [bk_b])
            CP("dve", arena[:, b * 512:(b + 1) * 512], bk[:, :], [bk_b], [regB[b]])
            t2, t2_b = t512()
            s.dma("sp", [(t2[:].rearrange("p (c f) -> p c f", c=2), cv[b].rearrange("(c m) f -> m c f", m=128))],
                  writes=[t2_b], owner=t2_b)
            CP("pool", arena[:, (16 + b) * 512:(17 + b) * 512], t2[:], [t2_b], [regB[16 + b]])

        def pTs3(ft):
            return pTs[:, ft, :].rearrange("p (b c) -> p b c", c=23)

        cpy_b = s.buf("poolcopy")

        def sample_hist_prep():
            for rt_ in range(2):
                tS, tS_b = t512()
                s.dma("sp", [(tS[0:120, 0:256], spst[rt_ * 8:(rt_ + 1) * 8].rearrange("b r c -> (b r) c"))], writes=[tS_b], owner=tS_b)
                bk, bk_b = pb()
                for ft in range(2):
                    TR(bk[:, ft * 128: ft * 128 + 120], tS[0:120, ft * 128:(ft + 1) * 128], ident[0:120, 0:120], [tS_b, ident_b], [bk_b])
                for ft in range(2):
                    CP("dve", pTs3(ft)[:, rt_ * 8:(rt_ + 1) * 8, 0:15],
                       bk[:, ft * 128: ft * 128 + 120].rearrange("p (b r) -> p b r", r=15), [bk_b], [pTs_b])
            s.dma("sp", [(pool_s[:, 0:7, :], spst[:, 8:15, :])], owner=cpy_b)

        def blk_info(kind, j):
            sample = kind == "sample"
            NT = 1 if sample else 4
            return sample, NT

        def P0a(kind, j):
            sample, NT = blk_info(kind, j)
            res = []
            for t in range(NT):
                src = xs if sample else xp[j * 512 + t * 128: j * 512 + (t + 1) * 128, :]
                T_, T_b = t1024()
                s.dma("sp", [(T_[:], src)], writes=[T_b], owner=T_b)
                res.append(norm_a(T_[:], T_b, (T_, T_b)))
            return res

        def P0b(kind, j, xns):
            sample, NT = blk_info(kind, j)
            for t in range(NT):
                norm_b(xns[t][0], xns[t][1], 0, hT, [[hT_bb[t][0]], [hT_bb[t][1]]], t * 128)

        def run_main(kind, j, nxt):
            sample = kind == "sample"
            NT = 1 if sample else 4
            cols = 128 * NT
            hT_all = [hT_bb[t][h] for t in range(NT) for h in range(2)]
            last = (not sample) and j == 3
            Rt, Rt_b = R, R_b
            rw = 128 if sample else 512
            C = CTX[kind]
            merged, merged_b, probs, probs_b = C["merged"], C["merged_b"], C["probs"], C["probs_b"]
            apre, apre_b, vh16, vh16_b = C["apre"], C["apre_b"], C["vh16"], C["vh16_b"]
            pooled, pooled_b, pm, pm_b = C["pooled"], C["pooled_b"], C["pm"], C["pm_b"]
            qT, qT_b, oT, oT_b = C["qT"], C["qT_b"], C["oT"], C["oT_b"]
            pTb_ = pTs_b if sample else pT_b
            route["pool2dve"] = (not sample) and j <= 1

            first_blk = (not sample) and j == 0

            def load_xres(t):
                src = xs if sample else xp[j * 512 + t * 128: j * 512 + (t + 1) * 128, :]
                s.dma("sp", [(xres[t][:], src)], writes=[xres_b[t]], owner=xres_b[t])

            for t in range(NT):
                if not (first_blk and t >= 2):
                    load_xres(t)

            w1, w1_b = wget("win1", hold=(3 if first_blk else 0))
            w13 = w1[:, :].rearrange("p (k c) -> p k c", k=8)
            gvs = []
            c0 = statcols(3 * 4)
            lnb = s.buf("lnstats%d" % c0)
            for t in range(NT):
                bk, bk_b = pb()
                for k in range(8):
                    MM(bk[:, :], hT[:, k, t * 128:(t + 1) * 128], w13[:, k, :], k == 0, k == 7, hT_bb[t] + [w1_b], [bk_b])
                gv, gv_b = t512()
                ACT(gv[:], bk[:, :], AF.Gelu_apprx_tanh, [bk_b], [gv_b])
                gvs.append((gv, gv_b))
            mvt = stats[:, c0:c0 + 8].rearrange("p (t c) -> p t c", c=2)
            c6 = statcols(6 * 4)
            for t in range(NT):
                gv, gv_b = gvs[t]
                st6b = s.buf("st6_%d_%d" % (c6, t))
                s.op("dve", lambda e, o=stats[:, c6 + 6 * t:c6 + 6 * t + 6], i=gv[:]: e.bn_stats(out=o, in_=i), [gv_b, stats_init], [st6b])
                s.op("dve", lambda e, o=mvt[:, t, :], i=stats[:, c6 + 6 * t:c6 + 6 * t + 6]: e.bn_aggr(out=o, in_=i), [st6b, stats_init], [lnb])
            TS("dve", stats[:, c0 + 8:c0 + 8 + NT], mvt[:, 0:NT, 1], EPS, None, ALU.add, None, [lnb], [lnb])
            w0, w0_b = wget("win0")
            w03 = w0[:, :].rearrange("p (k c) -> p k c", k=8)
            ugs = []
            for g in range(4):
                bk, bk_b = pb()
                for k in range(8):
                    MM(bk[:, 0:cols], w03[:, k, g * 128:(g + 1) * 128], hT[:, k, 0:cols], k == 0, k == 7, hT_all + [w0_b], [bk_b])
                ug, ug_b = t512()
                ACT(ug[:, 0:cols], bk[:, 0:cols], AF.Gelu_apprx_tanh, [bk_b], [ug_b])
                ugs.append((ug, ug_b))

            ACT(stats[:, c0 + 8:c0 + 8 + NT], stats[:, c0 + 8:c0 + 8 + NT], AF.Ln, [lnb], [lnb])
            ACT(stats[:, c0 + 8:c0 + 8 + NT], stats[:, c0 + 8:c0 + 8 + NT], AF.Exp, [lnb], [lnb], scale=-0.5)
            for t in range(NT):
                gv, gv_b = gvs[t]
                eng2 = "dve" if (t % 2 == 0) else "pool"
                TS("dve", gv[:], gv[:], mvt[:, t, 0:1], stats[:, c0 + 8 + t:c0 + 9 + t], ALU.subtract, ALU.mult, [gv_b, lnb], [gv_b])
                TT(eng2, gv[:], gv[:], gvb[:], ALU.mult, [gv_b, cst_b], [gv_b])
                if sample or (last and t == 3):
                    TT(eng2, gv[:], gv[:], bvb[:], ALU.add, [gv_b, cst_b], [gv_b])
                    s.dma("sp", [((cv_s if sample else cv_p)[:, :], gv[:])], reads=[gv_b], owner=gv_b)
                    CP(eng2, vh16[:, t, :], gv[:], [gv_b], [vh16_b[t]])
                else:
                    TT(eng2, vh16[:, t, :], gv[:], bvb[:], ALU.add, [gv_b, cst_b], [vh16_b[t]])

            w2, w2_b = wget("win2")
            w23 = w2[:, :].rearrange("p (k c) -> p k c", k=8)
            if not sample:
                if j == 0:
                    MEMSET("pool", pT[:, :, 0:16], 0.0, [pTb_])
                else:
                    CP("pool", pT[:, :, 1:16], pT[:, :, 513:528], [pTb_], [pTb_])
            for ft in range(2):
                bk, bk_b = pb()
                for k in range(8):
                    MM(bk[:, 0:cols], w23[:, k, ft * 128:(ft + 1) * 128], hT[:, k, 0:cols], k == 0, k == 7, hT_all + [w2_b], [bk_b])
                if sample:
                    CP("dve", pTs3(ft)[:, :, 15:23], bk[:, 0:128].rearrange("p (b l) -> p b l", l=8), [bk_b], [pTb_])
                else:
                    CP("dve", pT[:, ft, 16:528], bk[:, :], [bk_b], [pTb_])
            for ft in range(2):
                bk, bk_b = pb()
                for k in range(8):
                    MM(bk[:, 0:cols], w23[:, k, 256 + ft * 128: 256 + (ft + 1) * 128], hT[:, k, 0:cols], k == 0, k == 7,
                       hT_all + [w2_b], [bk_b])
                CP("act", qT[:, ft, 0:cols], bk[:, 0:cols], [bk_b], [qT_b])
            if sample or last:
                tt = 0 if sample else 3
                bk, bk_b = pb()
                for k in range(8):
                    MM(bk[:, 0:256], hT[:, k, tt * 128:(tt + 1) * 128], w23[:, k, 0:256], k == 0, k == 7, hT_bb[tt] + [w2_b], [bk_b])
                ptk, ptk_b = t512()
                CP("act", ptk[:, 0:256], bk[:, 0:256], [bk_b], [ptk_b])
                if sample:
                    s.dma("sp", [(pool_s[b, 7:15, :], ptk[b * 8:(b + 1) * 8, 0:256]) for b in range(16)], reads=[ptk_b], owner=ptk_b)
                else:
                    s.dma("sp", [(pool_p[:, :], ptk[113:128, 0:256])], reads=[ptk_b], owner=ptk_b)

            if first_blk:
                mem_prep_pe()
                load_xres(2)
                load_xres(3)

            if sample:
                def P3(ft):
                    return pTs3(ft)

                def V3(tile):
                    return tile[:, 0:368].rearrange("p (b c) -> p b c", c=23)

                def O3(ft):
                    return pooled[:, ft, 0:128].rearrange("p (b l) -> p b l", l=8)

                A, A_b = t1024()
                Bt, Bt_b = t1024()
                TT("pool", V3(A)[:, :, 1:23], P3(0)[:, :, 1:23], P3(0)[:, :, 0:22], ALU.add, [pTb_], [A_b])
                TT("pool", V3(Bt)[64:128, :, 3:23], V3(A)[64:128, :, 3:23], V3(A)[64:128, :, 1:21], ALU.add, [A_b], [Bt_b])
                STT("dve", O3(0)[0:64], V3(A)[0:64, :, 15:23], invw[0:64, 0:1], P3(0)[0:64, :, 15:23], ALU.mult, ALU.subtract,
                    [A_b, pTb_, cst2_b], [pooled_b])
                STT("dve", O3(0)[64:128], V3(Bt)[64:128, :, 15:23], invw[64:128, 0:1], P3(0)[64:128, :, 15:23], ALU.mult, ALU.subtract,
                    [Bt_b, pTb_, cst2_b], [pooled_b])
                A2, A2_b = t1024()
                B2, B2_b = t1024()
                TT("pool", V3(A2)[:, :, 1:23], P3(1)[:, :, 1:23], P3(1)[:, :, 0:22], ALU.add, [pTb_], [A2_b])
                TT("pool", V3(B2)[:, :, 3:23], V3(A2)[:, :, 3:23], V3(A2)[:, :, 1:21], ALU.add, [A2_b], [B2_b])
                C2, C2_b = t1024()
                TT("pool", V3(C2)[:, :, 7:23], V3(B2)[:, :, 7:23], V3(B2)[:, :, 3:19], ALU.add, [B2_b], [C2_b])
                D2, D2_b = t1024()
                TT("pool", V3(D2)[64:128, :, 15:23], V3(C2)[64:128, :, 15:23], V3(C2)[64:128, :, 7:15], ALU.add, [C2_b], [D2_b])
                STT("dve", O3(1)[0:64], V3(C2)[0:64, :, 15:23], invw[0:64, 1:2], P3(1)[0:64, :, 15:23], ALU.mult, ALU.subtract,
                    [C2_b, pTb_, cst2_b], [pooled_b])
                STT("dve", O3(1)[64:128], V3(D2)[64:128, :, 15:23], invw[64:128, 1:2], P3(1)[64:128, :, 15:23], ALU.mult, ALU.subtract,
                    [D2_b, pTb_, cst2_b], [pooled_b])
            else:
                fxs = []

                def fin(ft, lo, hi, S_, S_b):
                    STT("dve", pooled[lo:hi, ft, :], S_[lo:hi, 16:528], invw[lo:hi, ft:ft + 1], pT[lo:hi, ft, 16:528],
                        ALU.mult, ALU.subtract, [S_b, pTb_, cst2_b], [pooled_b])
                    if j == 0:
                        if not fxs:
                            fxs.append(t512())
                        fx, fx_b = fxs[0]
                        c_ = ft * 16
                        TT("dve", fx[lo:hi, c_:c_ + 15], S_[lo:hi, 16:31], invcnt[lo:hi, ft, 0:15], ALU.mult, [S_b, cst2_b], [fx_b])
                        TT("dve", pooled[lo:hi, ft, 0:15], fx[lo:hi, c_:c_ + 15], pT[lo:hi, ft, 16:31], ALU.subtract, [fx_b, pTb_], [pooled_b])

                A, A_b = t1024()
                Bt, Bt_b = t1024()
                TT("pool", A[:, 2:528], pT[:, 0, 2:528], pT[:, 0, 1:527], ALU.add, [pTb_], [A_b])
                TT("pool", Bt[64:128, 4:528], A[64:128, 4:528], A[64:128, 2:526], ALU.add, [A_b], [Bt_b])
                fin(0, 0, 64, A, A_b)
                fin(0, 64, 128, Bt, Bt_b)
                A2, A2_b = t1024()
                B2, B2_b = t1024()
                TT("dve", A2[:, 2:528], pT[:, 1, 2:528], pT[:, 1, 1:527], ALU.add, [pTb_], [A2_b])
                TT("dve", B2[:, 4:528], A2[:, 4:528], A2[:, 2:526], ALU.add, [A2_b], [B2_b])
                C2, C2_b = t1024()
                TT("dve", C2[:, 8:528], B2[:, 8:528], B2[:, 4:524], ALU.add, [B2_b], [C2_b])
                D2, D2_b = t1024()
                TT("dve", D2[64:128, 16:528], C2[64:128, 16:528], C2[64:128, 8:520], ALU.add, [C2_b], [D2_b])
                fin(1, 0, 64, C2, C2_b)
                fin(1, 64, 128, D2, D2_b)
            for hp in range(2):
                pr_b = probs_b[hp]
                if sample:
                    for hh in range(2):
                        bk, bk_b = pb()
                        for b in range(16):
                            kTb = arena[:, b * 512:(b + 1) * 512].rearrange("p (h m) -> p h m", h=2)
                            for mc in range(2):
                                cc = mc * 128 + b * 8
                                MM(bk[:, cc:cc + 8], kTb[hh * 64:(hh + 1) * 64, hp, mc * 128:(mc + 1) * 128],
                                   qT[hh * 64:(hh + 1) * 64, hp, b * 8:(b + 1) * 8], True, True, [regB[b], qT_b], [bk_b])
                        ACT(probs[:, hp * 4 + hh * 2:hp * 4 + hh * 2 + 2, 0:128], bk[:, 0:256].rearrange("p (i c) -> p i c", i=2), AF.Exp,
                            [bk_b], [pr_b], scale=0.125)
                else:
                    for hh in range(2):
                        for mc in range(2):
                            bk, bk_b = pb()
                            MM(bk[:, 0:cols], kT_p[hh * 64:(hh + 1) * 64, hp, mc * 128:(mc + 1) * 128],
                               qT[hh * 64:(hh + 1) * 64, hp, 0:cols], True, True, [kvp_b, qT_b], [bk_b])
                            ACT(probs[:, hp * 4 + hh * 2 + mc, 0:cols], bk[:, 0:cols], AF.Exp, [bk_b], [pr_b], scale=0.125)
            def pool_mix():
                for ft in range(2):
                    bk, bk_b = pb()
                    MM(bk[:, 0:cols], wpbd[:, ft, :], pooled[:, ft, 0:cols], True, True, [pooled_b, wpbd_b], [bk_b])
                    ACT(pm[:, ft, 0:cols], bk[:, 0:cols], AF.Identity, [bk_b, gcol_b], [pm_b], scale=pscol[:, ft:ft + 1])

            for hp in range(2):
                pr_b = probs_b[hp]
                bo, bo_b = pb()
                bd, bd_b = pb()
                if sample:
                    for b in range(16):
                        vb = arena[:, (16 + b) * 512:(17 + b) * 512].rearrange("p (c f) -> p c f", c=2)
                        for hh in range(2):
                            h = 2 * hp + hh
                            for mc in range(2):
                                MM(bo[hh * 64:(hh + 1) * 64, b * 8:(b + 1) * 8], vb[:, mc, h * 64:(h + 1) * 64],
                                   probs[:, hp * 4 + hh * 2 + mc, b * 8:(b + 1) * 8], mc == 0, mc == 1, [regB[16 + b], pr_b], [bo_b])
                    for hh in range(2):
                        for mc in range(2):
                            MM(bd[hh * 64:(hh + 1) * 64, 0:128], ones[:, 0:64], probs[:, hp * 4 + hh * 2 + mc, 0:128],
                               mc == 0, mc == 1, [pr_b, cst2_b], [bd_b])
                else:
                    for hh in range(2):
                        h = 2 * hp + hh
                        for mc in range(2):
                            MM(bo[hh * 64:(hh + 1) * 64, 0:cols], v_p[:, mc, h * 64:(h + 1) * 64],
                               probs[:, hp * 4 + hh * 2 + mc, 0:cols], mc == 0, mc == 1, [kvp_b, pr_b], [bo_b])
                    for hh in range(2):
                        for mc in range(2):
                            MM(bd[hh * 64:(hh + 1) * 64, 0:cols], ones[:, 0:64], probs[:, hp * 4 + hh * 2 + mc, 0:cols],
                               mc == 0, mc == 1, [pr_b, cst2_b], [bd_b])
                rd, rd_b = t512()
                ACT(rd[:, 0:cols], bd[:, 0:cols], AF.Ln, [bd_b], [rd_b])
                ACT(rd[:, 0:cols], rd[:, 0:cols], AF.Exp, [rd_b], [rd_b], scale=-1.0)
                TT("dve", oT[:, hp, 0:cols], bo[:, 0:cols], rd[:, 0:cols], ALU.mult, [bo_b, rd_b], [oT_b[hp]])

            wsT = wsT_s if sample else wsT_p
            bsx = bs_s if sample else bs_p
            wsTb_ = wsTs_b if sample else wsT_b
            bsb_ = csts_b if sample else cst_b
            for g in range(4):
                ug, ug_b = ugs[g]
                bk, bk_b = pb()
                for t in range(NT):
                    MM(bk[:, t * 128:(t + 1) * 128], vh16[:, t, g * 128:(g + 1) * 128], wsT[:, g, :], True, True,
                       [vh16_b[t], wsTb_], [bk_b])
                tm, tm_b = t512()
                for t in range(NT):
                    TT("dve", tm[:, t * 128:(t + 1) * 128], bk[:, t * 128:(t + 1) * 128], bsx[:, g, :], ALU.add, [bk_b, bsb_], [tm_b])
                TT("pool", apre[:, g, 0:cols], tm[:, 0:cols], ug[:, 0:cols], ALU.mult, [tm_b, ug_b], [apre_b[g]])

            for m in range(8):
                wm, wm_b = wget("mrg%d" % m)
                gb = []
                for br in range(3):
                    bk, bk_b = pb()
                    for k in range(8):
                        MM(bk[:, 0:cols], wm[:, (k * 3 + br) * 128:(k * 3 + br + 1) * 128], hT[:, k, 0:cols], k == 0, k == 7,
                           hT_all + [wm_b], [bk_b])
                    gb.append((bk, bk_b))
                if m == 0:
                    pool_mix()
                ob = []
                bk, bk_b = pb()
                for k in range(4):
                    MM(bk[:, 0:cols], wm[:, 3072 + k * 128: 3072 + (k + 1) * 128], apre[:, k, 0:cols], k == 0, k == 3,
                       [apre_b[k], wm_b], [bk_b])
                ob.append((bk, bk_b))
                bk, bk_b = pb()
                for k in range(2):
                    MM(bk[:, 0:cols], wm[:, 3584 + k * 128: 3584 + (k + 1) * 128], pm[:, k, 0:cols], k == 0, k == 1, [pm_b, wm_b], [bk_b])
                ob.append((bk, bk_b))
                bk, bk_b = pb()
                for k in range(2):
                    MM(bk[:, 0:cols], wm[:, 3840 + k * 128: 3840 + (k + 1) * 128], oT[:, k, 0:cols], k == 0, k == 1, [oT_b[k], wm_b], [bk_b])
                ob.append((bk, bk_b))
                ts_ = []
                for br in range(3):
                    sg, sg_b = t512()
                    ACT(sg[:, 0:cols], gb[br][0][:, 0:cols], AF.Sigmoid, [gb[br][1]], [sg_b])
                    TT("dve", sg[:, 0:cols], ob[br][0][:, 0:cols], sg[:, 0:cols], ALU.mult, [ob[br][1], sg_b], [sg_b])
                    ts_.append((sg, sg_b))
                TT("pool", ts_[0][0][:, 0:cols], ts_[0][0][:, 0:cols], ts_[1][0][:, 0:cols], ALU.add, [ts_[0][1], ts_[1][1]], [ts_[0][1]])
                TT("pool", merged[:, m, 0:cols], ts_[0][0][:, 0:cols], ts_[2][0][:, 0:cols], ALU.add, [ts_[0][1], ts_[2][1]], [merged_b[m]])

            wo_l = []
            for n in range(2):
                wo_, wo_b = wget("wo%d" % n, hold=n)
                wo_l.append((wo_[:, :].rearrange("p (k c) -> p k c", k=8), wo_b))
            r2s = []
            for t in range(NT):
                for n in range(2):
                    wo3, wo_b = wo_l[n]
                    bk, bk_b = pb()
                    for k in range(8):
                        MM(bk[:, :], merged[:, k, t * 128:(t + 1) * 128], wo3[:, k, :], k == 0, k == 7, [merged_b[k], wo_b], [bk_b])
                    TT("dve", xres[t][:, n * 512:(n + 1) * 512], bk[:, :], xres[t][:, n * 512:(n + 1) * 512], ALU.add,
                       [bk_b, xres_b[t]], [xres_b[t]])
                    if n == 0 and t >= 1:
                        norm_b(xres[t - 1], xres_b[t - 1], 1, hT, [[hT_bb[t - 1][0]], [hT_bb[t - 1][1]]], (t - 1) * 128)
                r2s.append(rms_rstd(xres[t][:], [xres_b[t]], 1024, power=-1.0))
            def ffn_evac(f, bk, bk_b):
                rl, rl_b = t512()
                ACT(rl[:, 0:cols], bk[:, 0:cols], AF.Relu, [bk_b], [rl_b])
                TT("pool" if (f % 2) else "dve", Rt[:, f * rw: f * rw + cols], rl[:, 0:cols], rl[:, 0:cols], ALU.mult, [rl_b], [Rt_b[f]])

            def sample_hooks(c):
                if nxt is not None and nxt[0] == "sample":
                    if c == 0:
                        sample_hist_prep()
                        sample_consts_ops()
                    sample_kv_prep(2 * c)
                    sample_kv_prep(2 * c + 1)

            split = NT == 4
            if split:
                sample_hooks(0)
                wu, wu_b = wget("wup0")
                wu3 = wu[:, :].rearrange("p (k c) -> p k c", k=8)
                bk0 = []

                def a_group(mt):
                    bk, bk_b = pb()
                    bk0.append((bk, bk_b))
                    for k in range(8):
                        MM(bk[:, 0:384], wu3[:, k, mt * 128:(mt + 1) * 128], hT[:, k, 0:384], k == 0, k == 7,
                           hT_all[0:6] + [wu_b], [bk_b])

                a_group(0)
                a_group(1)
            norm_b(xres[NT - 1], xres_b[NT - 1], 1, hT, [[hT_bb[NT - 1][0]], [hT_bb[NT - 1][1]]], (NT - 1) * 128)
            if split:
                a_group(2)
                a_group(3)
                for mt in range(4):
                    bk, bk_b = bk0[mt]
                    for k in range(8):
                        MM(bk[:, 384:512], wu3[:, k, mt * 128:(mt + 1) * 128], hT[:, k, 384:512], k == 0, k == 7,
                           hT_bb[3] + [wu_b], [bk_b])
                    ffn_evac(mt, bk, bk_b)

            nxt_xn = P0a(*nxt) if nxt is not None else None

            for c in range(8):
                if split and c == 0:
                    continue
                sample_hooks(c)
                wu, wu_b = wget("wup%d" % c)
                wu3 = wu[:, :].rearrange("p (k c) -> p k c", k=8)
                for mt in range(4):
                    f = c * 4 + mt
                    bk, bk_b = pb()
                    for k in range(8):
                        MM(bk[:, 0:cols], wu3[:, k, mt * 128:(mt + 1) * 128], hT[:, k, 0:cols], k == 0, k == 7, hT_all + [wu_b], [bk_b])
                    ffn_evac(f, bk, bk_b)

            if nxt is not None:
                P0b(nxt[0], nxt[1], nxt_xn)

            for kc in range(4):
                for n in range(2):
                    wd, wd_b = wget("wdn%d_%d" % (kc, n))
                    wd3 = wd[:, :].rearrange("p (k c) -> p k c", k=8)
                    for t in range(NT):
                        bi = 2 * t + n
                        for k in range(8):
                            f = kc * 8 + k
                            MM(banks[bi][:, :], Rt[:, f * rw + t * 128: f * rw + (t + 1) * 128], wd3[:, k, :],
                               kc == 0 and k == 0, kc == 3 and k == 7, [Rt_b[f], wd_b], [bank_b[bi]])
            rr["bank"] = 0
            for t in range(NT):
                for n in range(2):
                    bi = 2 * t + n
                    STT("dve", xres[t][:, n * 512:(n + 1) * 512], banks[bi][:, :], r2s[t][0], xres[t][:, n * 512:(n + 1) * 512],
                        ALU.mult, ALU.add, [bank_b[bi], xres_b[t], r2s[t][1]], [xres_b[t]])
                rstd, rb = rms_rstd(xres[t][:], [xres_b[t]], 1024)
                yt, yt_b = t1024()
                STT("dve", yt[:], xres[t][:], rstd, gfb[:], ALU.mult, ALU.mult, [xres_b[t], rb, cst_b], [yt_b])
                dst = ys if sample else yp[j * 512 + t * 128: j * 512 + (t + 1) * 128, :]
                s.dma("sp", [(dst, yt[:])], reads=[yt_b], owner=yt_b)
            route["pool2dve"] = False

        order = [("prompt", j) for j in range(4)] + [("sample", 0)]
        first_xn = P0a(*order[0])
        mem_prep_early()
        P0b(order[0][0], order[0][1], first_xn)
        for i, (kind, j) in enumerate(order):
            run_main(kind, j, order[i + 1] if i + 1 < len(order) else None)
            if i == 0:
                sample_consts()
        assert ws["pos"] == len(stream)
        s.emit()
    return nc


_NC = None


def kernel(x_prompt, x_sample, mem_prompt, cache_mem_k, cache_mem_v, state_pool,
           g_mix, w_in, g_v, b_v, w_s, b_s, w_pool, pool_scale, g_mem, w_kv,
           w_out_a, w_out_b, w_out_c, w_o, g_ffn, w_up, w_down, g_final):
    global _NC
    if _NC is None:
        _NC = build_program()
    nc = _NC
    f = lambda a: np.ascontiguousarray(np.asarray(a, dtype=np.float32))
    shared = {
        "g_mix": f(g_mix[0]), "w_in": f(w_in[0]), "g_v": f(g_v[0]), "b_v": f(b_v[0]), "w_s": f(w_s[0]),
        "b_s": f(b_s[0]), "w_pool": f(w_pool[0]), "pool_scale": f(pool_scale[0]), "g_mem": f(g_mem[0]),
        "w_kv": f(w_kv[0]), "w_out_a": f(w_out_a[0]), "w_out_b": f(w_out_b[0]), "w_out_c": f(w_out_c[0]),
        "w_o": f(w_o[0]), "g_ffn": f(g_ffn[0]), "w_up": f(w_up[0]), "w_down": f(w_down[0]), "g_final": f(g_final),
    }
    in_maps = []
    for c in range(8):
        d = dict(shared)
        d["xp"] = f(x_prompt[c])
        d["xs"] = f(np.asarray(x_sample)[16 * c:16 * (c + 1)].reshape(128, 1024))
        d["mem"] = f(mem_prompt[c])
        d["ck"] = f(np.asarray(cache_mem_k)[0, 16 * c:16 * (c + 1)].reshape(16, 256, 256))
        d["cv"] = f(np.asarray(cache_mem_v)[0, 16 * c:16 * (c + 1)].reshape(16, 256, 256))
        d["spst"] = f(np.asarray(state_pool)[0, 16 * c:16 * (c + 1)])
        in_maps.append(d)
    res = run_bass_kernel_spmd(nc, in_maps, core_ids=list(range(8)))
    r = res.results
    y_prompt = np.stack([r[c]["yp"] for c in range(8)], 0).astype(np.float32)
    y_sample = np.concatenate([r[c]["ys"].reshape(16, 8, 1024) for c in range(8)], 0).astype(np.float32)
    mem_k = np.stack([r[c]["mk"].reshape(256, 4, 64) for c in range(8)], 0)[None].astype(np.float32)
    mem_v = np.stack([r[c]["mv"].reshape(256, 4, 64) for c in range(8)], 0)[None].astype(np.float32)
    pool_p = np.stack([r[c]["pool_p"] for c in range(8)], 0)[None].astype(np.float32)
    pool_s = np.concatenate([r[c]["pool_s"] for c in range(8)], 0)[None].astype(np.float32)
    cv_p = np.stack([r[c]["cv_p"] for c in range(8)], 0)[None].astype(np.float32)
    cv_s = np.concatenate([r[c]["cv_s"].reshape(16, 8, 512) for c in range(8)], 0)[None].astype(np.float32)
    return (y_prompt, y_sample, mem_k, mem_v, pool_p, pool_s, cv_p, cv_s)
```
